# Optimizing a Trainium2 kernel written in Bass

```python
import math
import jax, jax.numpy as jnp
from jax import lax
import numpy as np


D_MODEL = 1024
BATCH = 8
SEQ = 4096
DEPTH = 2

EXPAND = 2
D_INNER = EXPAND * D_MODEL
N_MEM = 256
EPS = 1e-6
NEG = -1e30
BIG = 1e30

POOL_WINDOWS = (2, 4, 8, 16)
POOL_WIDTH = 3 * D_INNER // 8
POOL_GROUP = POOL_WIDTH // len(POOL_WINDOWS)

RET_HEADS = 4
RET_DK = 128
RET_DV = 192
RET_CHUNK = 128
ROPE_BASE = 10000.0

MEM_HEADS = 4
MEM_DH = 128
MEM_WIDTH = MEM_HEADS * MEM_DH

NSA_HEADS = 12
NSA_KV = 2
NSA_HPG = NSA_HEADS // NSA_KV
NSA_DH = 128
CMP_BLOCK = 32
CMP_STRIDE = 16
CMP_HIDDEN = 256
SLC_BLOCK = 64
SLC_TOPK = 8
WINDOW = 512
NSA_QBLOCK = 32

REL_BUCKETS = 32
REL_MAX_DIST = 128

EVEN_SPLITS = (POOL_WIDTH, RET_HEADS * RET_DK, RET_HEADS * RET_DK, RET_HEADS * RET_DV, MEM_WIDTH, D_INNER)
ODD_SPLITS = (NSA_HEADS * NSA_DH,) + (NSA_KV * NSA_DH,) * 6 + (3 * NSA_HEADS, MEM_WIDTH, D_INNER)
EVEN_COLS = sum(EVEN_SPLITS)
ODD_COLS = sum(ODD_SPLITS)

kernel_name = 'hybrid_pool_retention_nsa_memory'


def rmsnorm(x, g):
    xf = x.astype(jnp.float32)
    y = xf * lax.rsqrt(jnp.mean(xf * xf, axis=-1, keepdims=True) + EPS)
    return (y * g.astype(jnp.float32)).astype(x.dtype)


def split_cols(z, sizes):
    return jnp.split(z, np.cumsum(sizes)[:-1].tolist(), axis=-1)


def t5_bucket(rel):
    n = jnp.maximum(rel, 0)
    max_exact = REL_BUCKETS // 2
    nf = jnp.maximum(n, 1).astype(jnp.float32)
    large = max_exact + (jnp.log(nf / max_exact) / math.log(REL_MAX_DIST / max_exact)
                         * (REL_BUCKETS - max_exact)).astype(jnp.int32)
    large = jnp.minimum(large, REL_BUCKETS - 1)
    return jnp.where(n < max_exact, n, large)


def pool_mixer(u, w_grp, scale):
    B_, S, W = u.shape
    uf = u.astype(jnp.float32)
    cs = jnp.concatenate([jnp.zeros((B_, 1, W), jnp.float32), jnp.cumsum(uf, axis=1)], axis=1)
    t = jnp.arange(S)
    outs = []
    for gi, w in enumerate(POOL_WINDOWS):
        sl = slice(gi * POOL_GROUP, (gi + 1) * POOL_GROUP)
        lo = jnp.maximum(t + 1 - w, 0)
        cnt = (t + 1 - lo).astype(jnp.float32)[None, :, None]
        outs.append((cs[:, 1:, sl] - cs[:, lo, sl]) / cnt - uf[..., sl])
    pooled = jnp.stack(outs, axis=2)
    y = jnp.einsum('bsgc,gcd->bsgd', pooled, w_grp.astype(jnp.float32)).reshape(B_, S, W)
    return y * scale.astype(jnp.float32)


def rotary(x):
    S, Dh = x.shape[-2], x.shape[-1]
    half = Dh // 2
    inv = ROPE_BASE ** (-jnp.arange(half, dtype=jnp.float32) / half)
    ang = jnp.arange(S, dtype=jnp.float32)[:, None] * inv[None, :]
    cos, sin = jnp.cos(ang), jnp.sin(ang)
    x1 = x[..., :half].astype(jnp.float32)
    x2 = x[..., half:].astype(jnp.float32)
    return jnp.concatenate([x1 * cos - x2 * sin, x1 * sin + x2 * cos], axis=-1)


def retention(q, k, v):
    B_, S, _ = q.shape
    H, C = RET_HEADS, RET_CHUNK
    N = S // C
    qh = rotary(q.reshape(B_, S, H, RET_DK).transpose(0, 2, 1, 3)) * (RET_DK ** -0.5)
    kh = rotary(k.reshape(B_, S, H, RET_DK).transpose(0, 2, 1, 3))
    vh = v.reshape(B_, S, H, RET_DV).transpose(0, 2, 1, 3).astype(jnp.float32)
    log_g = jnp.log(1.0 - jnp.exp2(-5.0 - jnp.arange(H, dtype=jnp.float32)))
    n = jnp.arange(C, dtype=jnp.float32)
    diff = n[:, None] - n[None, :]
    decay_in = jnp.where(diff >= 0, jnp.exp(log_g[:, None, None] * jnp.maximum(diff, 0.0)), 0.0)
    xi = jnp.exp(log_g[:, None] * (n + 1.0))
    zeta = jnp.exp(log_g[:, None] * (C - 1.0 - n))
    g_chunk = jnp.exp(log_g * C)
    qc = qh.reshape(B_, H, N, C, RET_DK)
    kc = kh.reshape(B_, H, N, C, RET_DK)
    vc = vh.reshape(B_, H, N, C, RET_DV)
    att = jnp.einsum('bhncd,bhnmd->bhncm', qc, kc) * decay_in[None, :, None]
    o_inner = jnp.einsum('bhncm,bhnmv->bhncv', att, vc)
    kv = jnp.einsum('bhnmd,bhnmv->bhndv', kc * zeta[None, :, None, :, None], vc)

    def step(R, kv_n):
        return R * g_chunk[None, :, None, None] + kv_n, R

    _, R_prev = lax.scan(step, jnp.zeros((B_, H, RET_DK, RET_DV), jnp.float32), kv.transpose(2, 0, 1, 3, 4))
    R_prev = R_prev.transpose(1, 2, 0, 3, 4)
    o_cross = jnp.einsum('bhncd,bhndv->bhncv', qc * xi[None, :, None, :, None], R_prev)
    o = (o_inner + o_cross).reshape(B_, H, S, RET_DV)
    mu = jnp.mean(o, axis=-1, keepdims=True)
    var = jnp.mean((o - mu) ** 2, axis=-1, keepdims=True)
    o = (o - mu) * lax.rsqrt(var + EPS)
    return o.transpose(0, 2, 1, 3).reshape(B_, S, H * RET_DV)


def mem_attention(xq, mem_n, w_mem_kv):
    B_, S, _ = xq.shape
    M = mem_n.shape[1]
    mk, mv = jnp.split(mem_n @ w_mem_kv, 2, axis=-1)
    mk = mk.reshape(B_, M, MEM_HEADS, MEM_DH)
    mv = mv.reshape(B_, M, MEM_HEADS, MEM_DH)
    qh = xq.reshape(B_, S, MEM_HEADS, MEM_DH).astype(jnp.float32) * (MEM_DH ** -0.5)
    p = jax.nn.softmax(jnp.einsum('bshd,bmhd->bhsm', qh, mk), axis=-1)
    o = jnp.einsum('bhsm,bmhd->bshd', p, mv)
    return o.reshape(B_, S, MEM_WIDTH)


def compress(k, pe, w1, b1, w2):
    B_, S = k.shape[:2]
    n_cmp = (S - CMP_BLOCK) // CMP_STRIDE + 1
    idx = jnp.arange(n_cmp)[:, None] * CMP_STRIDE + jnp.arange(CMP_BLOCK)[None, :]
    blocks = k[:, idx] + pe[:, None, :]
    flat = blocks.transpose(0, 1, 3, 2, 4).reshape(B_, n_cmp, NSA_KV, CMP_BLOCK * NSA_DH)
    return jax.nn.silu(flat @ w1 + b1) @ w2


def cmp_to_slc_matrix(n_cmp, n_slc):
    cst = np.arange(n_cmp)[:, None] * CMP_STRIDE
    sst = np.arange(n_slc)[None, :] * SLC_BLOCK
    ov = np.clip(np.minimum(cst + CMP_BLOCK, sst + SLC_BLOCK) - np.maximum(cst, sst), 0, None)
    return jnp.asarray(ov / CMP_STRIDE, dtype=jnp.float32)


def nsa_attention(q, kc, vc, ks, vs, kw, vw, gate_logits, cmp_pe, cmp_w1, cmp_b1, cmp_w2, rel_bias):
    B_, S = q.shape[:2]
    QB = NSA_QBLOCK
    qh = q.reshape(B_, S, NSA_KV, NSA_HPG, NSA_DH).transpose(0, 2, 3, 1, 4).astype(jnp.float32) * (NSA_DH ** -0.5)

    def kv_heads(t):
        return t.reshape(B_, S, NSA_KV, NSA_DH)

    kcmp = compress(kv_heads(kc), cmp_pe[0], cmp_w1[0], cmp_b1[0], cmp_w2[0]).transpose(0, 2, 1, 3)
    vcmp = compress(kv_heads(vc), cmp_pe[1], cmp_w1[1], cmp_b1[1], cmp_w2[1]).transpose(0, 2, 1, 3)
    n_cmp = kcmp.shape[2]
    n_slc = S // SLC_BLOCK
    kslc = kv_heads(ks).reshape(B_, n_slc, SLC_BLOCK, NSA_KV, NSA_DH).transpose(0, 3, 1, 2, 4)
    vslc = kv_heads(vs).reshape(B_, n_slc, SLC_BLOCK, NSA_KV, NSA_DH).transpose(0, 3, 1, 2, 4)
    pad = ((0, 0), (0, 0), (WINDOW, 0), (0, 0))
    kwin = jnp.pad(kv_heads(kw).transpose(0, 2, 1, 3), pad)
    vwin = jnp.pad(kv_heads(vw).transpose(0, 2, 1, 3), pad)
    gates = jax.nn.sigmoid(gate_logits.astype(jnp.float32)).reshape(B_, S, 3, NSA_KV, NSA_HPG).transpose(0, 2, 3, 4, 1)
    overlap = cmp_to_slc_matrix(n_cmp, n_slc)
    cend = jnp.arange(n_cmp) * CMP_STRIDE + CMP_BLOCK - 1
    table = rel_bias.astype(jnp.float32)
    table_g = table.reshape(REL_BUCKETS, NSA_KV, NSA_HPG)
    k_sel = min(SLC_TOPK, n_slc)
    b_idx = jnp.arange(B_)[:, None, None, None]
    g_idx = jnp.arange(NSA_KV)[None, :, None, None]

    def head_bias(rel):
        return table[t5_bucket(rel)].transpose(2, 0, 1).reshape(NSA_KV, NSA_HPG, rel.shape[0], rel.shape[1])

    def masked_softmax(s, mask):
        return jax.nn.softmax(jnp.where(mask, s, NEG), axis=-1)

    def block(s0):
        tpos = s0 + jnp.arange(QB)
        qb = lax.dynamic_slice_in_dim(qh, s0, QB, axis=3)
        gb = lax.dynamic_slice_in_dim(gates, s0, QB, axis=4)
        rel_c = tpos[:, None] - cend[None, :]
        sc = jnp.einsum('bghqd,bgjd->bghqj', qb, kcmp) + head_bias(rel_c)
        pc = masked_softmax(sc, rel_c >= 0) * (tpos >= CMP_BLOCK - 1).astype(jnp.float32)[:, None]
        oc = jnp.einsum('bghqj,bgjd->bghqd', pc, vcmp)
        imp = jnp.einsum('bghqj,js->bgqs', pc, overlap)
        sblk = jnp.arange(n_slc)[None, :]
        cur = (tpos // SLC_BLOCK)[:, None]
        forced = (sblk == 0) | (sblk == cur) | (sblk == cur - 1)
        future = sblk * SLC_BLOCK > tpos[:, None]
        imp = jnp.where(forced, BIG, jnp.where(future, -BIG, imp))
        _, sel = lax.top_k(imp, k_sel)
        kg = kslc[b_idx, g_idx, sel].reshape(B_, NSA_KV, QB, k_sel * SLC_BLOCK, NSA_DH)
        vg = vslc[b_idx, g_idx, sel].reshape(B_, NSA_KV, QB, k_sel * SLC_BLOCK, NSA_DH)
        kpos_s = (sel[..., None] * SLC_BLOCK + jnp.arange(SLC_BLOCK)).reshape(B_, NSA_KV, QB, k_sel * SLC_BLOCK)
        rel_s = tpos[None, None, :, None] - kpos_s
        bias_s = table_g[t5_bucket(rel_s), g_idx].transpose(0, 1, 4, 2, 3)
        ss = jnp.einsum('bghqd,bgqkd->bghqk', qb, kg) + bias_s
        ps = masked_softmax(ss, (rel_s >= 0)[:, :, None])
        osel = jnp.einsum('bghqk,bgqkd->bghqd', ps, vg)
        kwb = lax.dynamic_slice_in_dim(kwin, s0, WINDOW + QB, axis=2)
        vwb = lax.dynamic_slice_in_dim(vwin, s0, WINDOW + QB, axis=2)
        kpos_w = s0 - WINDOW + jnp.arange(WINDOW + QB)
        rel_w = tpos[:, None] - kpos_w[None, :]
        mask_w = (rel_w >= 0) & (rel_w < WINDOW) & (kpos_w >= 0)[None, :]
        sw = jnp.einsum('bghqd,bgkd->bghqk', qb, kwb) + head_bias(rel_w)
        pw = masked_softmax(sw, mask_w)
        ow = jnp.einsum('bghqk,bgkd->bghqd', pw, vwb)
        return gb[:, 0, ..., None] * oc + gb[:, 1, ..., None] * osel + gb[:, 2, ..., None] * ow

    outs = lax.map(block, jnp.arange(S // QB) * QB)
    return outs.transpose(1, 0, 4, 2, 3, 5).reshape(B_, S, NSA_HEADS * NSA_DH)


def even_layer(h, mem_n, g, w_in, pool_w, pool_scale, w_mem_kv, w_out):
    u = rmsnorm(h, g)
    za, rq, rk, rv, xq, gate = split_cols(u @ w_in, EVEN_SPLITS)
    a = pool_mixer(za, pool_w, pool_scale)
    r = retention(rq, rk, rv)
    m = mem_attention(xq, mem_n, w_mem_kv)
    y = jnp.concatenate([a, r, m], axis=-1) * jax.nn.silu(gate.astype(jnp.float32))
    return h + (y @ w_out).astype(h.dtype)


def odd_layer(h, mem_n, g, w_in, cmp_pe, cmp_w1, cmp_b1, cmp_w2, w_mem_kv, w_out, rel_bias):
    u = rmsnorm(h, g)
    q, kc, vc, ks, vs, kw, vw, gl, xq, gate = split_cols(u @ w_in, ODD_SPLITS)
    c = nsa_attention(q, kc, vc, ks, vs, kw, vw, gl, cmp_pe, cmp_w1, cmp_b1, cmp_w2, rel_bias)
    m = mem_attention(xq, mem_n, w_mem_kv)
    y = jnp.concatenate([c, m], axis=-1) * jax.nn.silu(gate.astype(jnp.float32))
    return h + (y @ w_out).astype(h.dtype)


def setup_inputs(seed: int = 0) -> dict:
    key = jax.random.key(seed)
    ks = jax.random.split(key, 20)
    n_even = (DEPTH + 1) // 2
    n_odd = DEPTH // 2

    def nrm(k, shape, scale):
        return scale * jax.random.normal(k, shape, jnp.float32)

    return {
        'x': nrm(ks[0], (BATCH, SEQ, D_MODEL), 1.0),
        'mem': nrm(ks[1], (BATCH, N_MEM, D_MODEL), 1.0),
        'norm_g': 1.0 + nrm(ks[2], (DEPTH, D_MODEL), 0.05),
        'final_g': 1.0 + nrm(ks[3], (D_MODEL,), 0.05),
        'mem_norm_g': 1.0 + nrm(ks[4], (D_MODEL,), 0.05),
        'rel_bias': nrm(ks[5], (REL_BUCKETS, NSA_HEADS), 0.5),
        'ev_w_in': nrm(ks[6], (n_even, D_MODEL, EVEN_COLS), D_MODEL ** -0.5),
        'ev_pool_w': nrm(ks[7], (n_even, len(POOL_WINDOWS), POOL_GROUP, POOL_GROUP), POOL_GROUP ** -0.5),
        'ev_pool_scale': 1.0 + nrm(ks[8], (n_even, POOL_WIDTH), 0.1),
        'ev_w_mem_kv': nrm(ks[9], (n_even, D_MODEL, 2 * MEM_WIDTH), D_MODEL ** -0.5),
        'ev_w_out': nrm(ks[10], (n_even, D_INNER, D_MODEL), D_INNER ** -0.5),
        'od_w_in': nrm(ks[11], (n_odd, D_MODEL, ODD_COLS), D_MODEL ** -0.5),
        'od_cmp_pe': nrm(ks[12], (n_odd, 2, CMP_BLOCK, NSA_DH), 0.1),
        'od_cmp_w1': nrm(ks[13], (n_odd, 2, CMP_BLOCK * NSA_DH, CMP_HIDDEN), (CMP_BLOCK * NSA_DH) ** -0.5),
        'od_cmp_b1': nrm(ks[14], (n_odd, 2, CMP_HIDDEN), 0.01),
        'od_cmp_w2': nrm(ks[15], (n_odd, 2, CMP_HIDDEN, NSA_DH), CMP_HIDDEN ** -0.5),
        'od_w_mem_kv': nrm(ks[16], (n_odd, D_MODEL, 2 * MEM_WIDTH), D_MODEL ** -0.5),
        'od_w_out': nrm(ks[17], (n_odd, D_INNER, D_MODEL), D_INNER ** -0.5),
    }


def reference(x, mem, norm_g, final_g, mem_norm_g, rel_bias, ev_w_in, ev_pool_w, ev_pool_scale, ev_w_mem_kv, ev_w_out,
              od_w_in, od_cmp_pe, od_cmp_w1, od_cmp_b1, od_cmp_w2, od_w_mem_kv, od_w_out):
    mem_n = rmsnorm(mem, mem_norm_g)
    h = x
    for i in range(DEPTH):
        j = i // 2
        if i % 2 == 0:
            h = even_layer(h, mem_n, norm_g[i], ev_w_in[j], ev_pool_w[j], ev_pool_scale[j], ev_w_mem_kv[j], ev_w_out[j])
        else:
            h = odd_layer(h, mem_n, norm_g[i], od_w_in[j], od_cmp_pe[j], od_cmp_w1[j], od_cmp_b1[j], od_cmp_w2[j],
                          od_w_mem_kv[j], od_w_out[j], rel_bias)
    return rmsnorm(h, final_g)
```

```python
import math
from contextlib import ExitStack, contextmanager

import numpy as np
import concourse.bass as bass
import concourse.mybir as mybir
from concourse.bass_utils import run_bass_kernel_spmd

F32 = mybir.dt.float32
BF16 = mybir.dt.bfloat16
AF = mybir.ActivationFunctionType
ALU = mybir.AluOpType
AX = mybir.AxisListType

S = 4096
D = 1024
NT = S // 128
NEG = -30000.0
BIGV = 1.0e30
FOFF = 2176
FLEN = 4608
EPS = 1e-6


class Buf:
    __slots__ = ("name", "ws", "rs", "excl")

    def __init__(self, name, excl=False):
        self.name = name
        self.ws = []
        self.rs = []
        self.excl = excl


class V:
    __slots__ = ("ap", "bufs")

    def __init__(self, ap, bufs):
        self.ap = ap
        self.bufs = tuple(bufs)

    def __getitem__(self, idx):
        return V(self.ap[idx], self.bufs)

    def re(self, s, **kw):
        return V(self.ap.rearrange(s, **kw), self.bufs)

    def bcast(self, shape):
        return V(self.ap.to_broadcast(list(shape)), self.bufs)

    def wb(self, *bufs):
        return V(self.ap, bufs)

    def bf(self):
        return V(self.ap.bitcast(BF16), self.bufs)

    @property
    def shape(self):
        return self.ap.shape


class Inst:
    __slots__ = ("eng", "fn", "deps", "dma", "signal", "tick", "dma_n", "idx")


COMPUTE = ("pe", "act", "dve", "pool")
DMAQ = ("sp", "pool", "act")
ALLENG = ("sp", "pe", "act", "dve", "pool")
RING = 24


class Prog:
    def __init__(self, nc):
        self.nc = nc
        self.insts = []
        self.eng_obj = {"pe": nc.tensor, "act": nc.scalar, "dve": nc.vector,
                        "pool": nc.gpsimd, "sp": nc.sync}
        self.nbuf = 0
        self.stacks = []
        self.last = {e: None for e in ALLENG}
        self.dma_since = []

    def buf(self, name=None, excl=False):
        self.nbuf += 1
        return Buf(name or f"b{self.nbuf}", excl)

    @contextmanager
    def scope(self):
        es = ExitStack()
        self.stacks.append(es)
        try:
            yield
        finally:
            self.barrier()
            self.stacks.pop()
            es.close()

    def sbuf(self, name, shape, dtype):
        t = self.stacks[-1].enter_context(self.nc.sbuf_tensor(name, list(shape), dtype))
        return V(t[tuple(slice(None) for _ in shape)], [self.buf(name)])

    def psum(self, name, shape, dtype):
        t = self.nc.alloc_psum_tensor(name, list(shape), dtype)
        return V(t[tuple(slice(None) for _ in shape)], [self.buf(name, excl=True)])

    def dram(self, name, shape, dtype, kind="Internal"):
        t = self.nc.dram_tensor(name, list(shape), dtype, kind=kind)
        return V(t.ap(), [self.buf(name)])

    def add(self, eng, fn, reads, writes, dma=False, waw=True):
        idx = len(self.insts)
        ins = Inst()
        ins.eng, ins.fn, ins.dma, ins.signal, ins.idx = eng, fn, dma, False, idx
        ins.tick = None
        ins.dma_n = None
        deps = set()
        rb, wbufs = [], []
        for v in reads:
            for b in v.bufs:
                (wbufs if b.excl else rb).append(b)
        for v in writes:
            wbufs.extend(v.bufs)
        for b in rb:
            for w in b.ws:
                deps.add((w, "raw"))
        for b in wbufs:
            if waw or b.excl:
                for w in b.ws:
                    deps.add((w, "raw" if b.excl else "waw"))
            for r in b.rs:
                deps.add((r, "war"))
        for b in rb:
            b.rs.append(idx)
        for b in wbufs:
            if waw or b.excl:
                b.ws = [idx]
                b.rs = []
            else:
                b.ws.append(idx)
        keep = {}
        for d, kind in deps:
            if d == idx:
                continue
            di = self.insts[d]
            if not di.dma and not dma and di.eng == eng:
                if eng == "pe":
                    continue
                if kind != "raw":
                    continue
            if di.dma:
                keep[("dma", d)] = d
            else:
                k = ("eng", di.eng)
                if k not in keep or keep[k] < d:
                    keep[k] = d
        ins.deps = sorted(keep.values())
        for d in ins.deps:
            self.insts[d].signal = True
        self.insts.append(ins)
        if dma:
            self.dma_since.append(idx)
        else:
            self.last[eng] = idx
        return ins

    def barrier(self):
        lasts = [v for v in self.last.values() if v is not None]
        dmas = list(self.dma_since)
        self.dma_since = []
        for e in ALLENG:
            ins = Inst()
            ins.eng, ins.fn, ins.dma, ins.signal, ins.idx = e, None, False, False, len(self.insts)
            ins.tick = None
            ins.dma_n = None
            ins.deps = sorted(set([d for d in lasts if self.insts[d].eng != e] + dmas))
            for d in ins.deps:
                self.insts[d].signal = True
            self.insts.append(ins)

    def mm(self, out, lhsT, rhs, start=True, stop=True, **kw):
        nc = self.nc
        return self.add("pe", lambda: nc.tensor.matmul(out.ap, lhsT.ap, rhs.ap, start=start, stop=stop, **kw),
                        [lhsT, rhs], [out])

    def tr(self, out, in_, ident):
        nc = self.nc
        return self.add("pe", lambda: nc.tensor.transpose(out.ap, in_.ap, ident.ap), [in_, ident], [out])

    def act(self, out, in_, func, bias=None, scale=None, accum_out=None):
        nc = self.nc
        kw = {}
        reads = [in_]
        writes = [out]
        if bias is not None:
            if isinstance(bias, V):
                kw["bias"] = bias.ap
                reads.append(bias)
            else:
                kw["bias"] = bias
        if scale is not None:
            if isinstance(scale, V):
                kw["scale"] = scale.ap
                reads.append(scale)
            else:
                kw["scale"] = scale
        if accum_out is not None:
            kw["accum_out"] = accum_out.ap
            writes.append(accum_out)
        return self.add("act", lambda: nc.scalar.activation(out.ap, in_.ap, func, **kw), reads, writes)

    def copy(self, eng, out, in_):
        e = self.eng_obj[eng]
        if eng == "act":
            return self.add("act", lambda: e.copy(out.ap, in_.ap), [in_], [out])
        return self.add(eng, lambda: e.tensor_copy(out.ap, in_.ap), [in_], [out])

    def tt(self, eng, out, in0, in1, op):
        e = self.eng_obj[eng]
        return self.add(eng, lambda: e.tensor_tensor(out.ap, in0.ap, in1.ap, op), [in0, in1], [out])

    def ts(self, eng, out, in0, s1, op0, s2=None, op1=None):
        e = self.eng_obj[eng]
        reads = [in0]
        a1, a2 = s1, s2
        if isinstance(s1, V):
            reads.append(s1)
            a1 = s1.ap
        if isinstance(s2, V):
            reads.append(s2)
            a2 = s2.ap
        kw = {}
        if op1 is not None:
            kw["op1"] = op1
        return self.add(eng, lambda: e.tensor_scalar(out.ap, in0.ap, a1, a2, op0, **kw), reads, [out])

    def stt(self, out, in0, scalar, in1, op0, op1):
        e = self.nc.vector
        reads = [in0, in1]
        sc = scalar
        if isinstance(scalar, V):
            reads.append(scalar)
            sc = scalar.ap
        return self.add("dve", lambda: e.scalar_tensor_tensor(out.ap, in0.ap, sc, in1.ap, op0, op1), reads, [out])

    def memset(self, eng, out, val):
        e = self.eng_obj[eng]
        return self.add(eng, lambda: e.memset(out.ap, val), [], [out])

    def max8(self, out, in_):
        nc = self.nc
        return self.add("dve", lambda: nc.vector.max(out.ap, in_.ap), [in_], [out])

    def recip(self, out, in_):
        nc = self.nc
        return self.add("dve", lambda: nc.vector.reciprocal(out.ap, in_.ap), [in_], [out])

    def bn_stats(self, out, in_):
        nc = self.nc
        return self.add("dve", lambda: nc.vector.bn_stats(out.ap, in_.ap), [in_], [out])

    def bn_aggr(self, out, in_):
        nc = self.nc
        return self.add("dve", lambda: nc.vector.bn_aggr(out.ap, in_.ap), [in_], [out])

    def dma(self, q, out, in_, waw=True, **kw):
        e = self.eng_obj[q]
        return self.add(q, lambda: e.dma_start(out.ap, in_.ap, **kw), [in_], [out], dma=True, waw=waw)

    def emit(self):
        nc = self.nc
        insts = self.insts
        tick = {e: 0 for e in COMPUTE}
        dman = {q: 0 for q in DMAQ}
        for ins in insts:
            if ins.dma:
                ins.dma_n = dman[ins.eng]
                dman[ins.eng] += 1
            elif ins.signal:
                tick[ins.eng] += 1
                ins.tick = tick[ins.eng]
        sems = {e: nc.alloc_semaphore(f"sem_{e}") for e in COMPUTE}
        rings = {}
        for q in DMAQ:
            if dman[q]:
                rings[q] = [nc.alloc_semaphore(f"ring_{q}_{i}") for i in range(min(RING, dman[q]))]
        per_eng = {e: [] for e in ALLENG}
        for ins in insts:
            per_eng[ins.eng].append(ins)

        def run_engine(ename):
            eobj = self.eng_obj[ename]
            known = {}

            def wait(sem, key, val):
                if known.get(key, 0) >= val:
                    return
                known[key] = val
                eobj.wait_ge(sem, val)

            for ins in per_eng[ename]:
                for d in ins.deps:
                    di = insts[d]
                    if di.dma:
                        r = rings[di.eng]
                        k = di.dma_n % len(r)
                        wait(r[k], ("ring", di.eng, k), 16 * (di.dma_n // len(r) + 1))
                    elif di.tick is not None:
                        wait(sems[di.eng], ("eng", di.eng), di.tick)
                if ins.fn is None:
                    continue
                if ins.dma:
                    r = rings[ins.eng]
                    k = ins.dma_n % len(r)
                    gen = ins.dma_n // len(r)
                    if gen > 0:
                        wait(r[k], ("ring", ins.eng, k), 16 * gen)
                    ins.fn().then_inc(r[k], 16)
                else:
                    o = ins.fn()
                    if ins.signal:
                        o.then_inc(sems[ename], 1)
            if ename in rings:
                r = rings[ename]
                n = dman[ename]
                for k in range(len(r)):
                    cnt = (n - k + len(r) - 1) // len(r)
                    if cnt > 0:
                        wait(r[k], ("ring", ename, k), 16 * cnt)

        with nc.Block() as block:
            @block.sync
            def _(e):
                run_engine("sp")

            @block.tensor
            def _(e):
                run_engine("pe")

            @block.scalar
            def _(e):
                run_engine("act")

            @block.vector
            def _(e):
                run_engine("dve")

            @block.gpsimd
            def _(e):
                run_engine("pool")
        return nc


def _bucket(n):
    n = max(int(n), 0)
    if n < 16:
        return n
    nf = np.float32(max(n, 1))
    v = np.log(nf / np.float32(16.0)) / np.float32(math.log(128 / 16)) * np.float32(16.0)
    return min(16 + int(np.float32(v)), 31)


_CONSTS = None


def make_consts():
    global _CONSTS
    if _CONSTS is not None:
        return _CONSTS
    f32 = np.float32
    c = {}
    c["c_ident"] = np.eye(128, dtype=f32)
    c["c_anti"] = np.ascontiguousarray(np.eye(128, dtype=f32)[::-1])
    half = 64
    inv = (np.float32(10000.0) ** (-np.arange(half, dtype=f32) / np.float32(half))).astype(f32)
    ang = (np.arange(S, dtype=f32)[:, None] * inv[None, :]).astype(f32)
    cs = np.cos(ang).astype(f32).reshape(NT, 128, half).transpose(1, 0, 2)
    sn = np.sin(ang).astype(f32).reshape(NT, 128, half).transpose(1, 0, 2)
    c["c_cos"] = np.ascontiguousarray(cs)
    c["c_sin"] = np.ascontiguousarray(sn)
    gam = 1.0 - 2.0 ** (-5.0 - np.arange(4))
    scale = 128.0 ** -0.5
    m = np.arange(128)[:, None, None]
    cc = np.arange(128)[None, None, :]
    gh = gam[None, :, None]
    decT = np.where(cc >= m, scale * gh ** np.maximum(cc - m, 0), 0.0)
    c["c_decT"] = decT.astype(f32)
    xi = scale * gh ** (cc + 1.0)
    c["c_xi"] = np.ascontiguousarray(np.broadcast_to(xi, (128, 4, 128))).astype(f32)
    c["c_zeta"] = (gam[None, :] ** (127.0 - np.arange(128)[:, None])).astype(f32)
    c["gchunk"] = [float(g ** 128) for g in gam]
    A = np.zeros((128, 12, 128), f32)
    tp = np.arange(128)[:, None]
    t = np.arange(128)[None, :]
    for wi, w in enumerate((2, 4, 8, 16)):
        cur = np.where((tp > t - w) & (tp <= t), 1.0 / w, 0.0) - (tp == t)
        prev = np.where(tp > 128 + t - w, 1.0 / w, 0.0)
        cnt = np.minimum(t + 1, w)
        first = np.where((tp >= np.maximum(0, t - w + 1)) & (tp <= t), 1.0 / cnt, 0.0) - (tp == t)
        A[:, wi * 3 + 0, :] = cur
        A[:, wi * 3 + 1, :] = prev
        A[:, wi * 3 + 2, :] = first
    c["c_poolA"] = A
    n_cmp, n_slc = 255, 64
    cst = np.arange(n_cmp)[:, None] * 16
    sst = np.arange(n_slc)[None, :] * 64
    ov = np.clip(np.minimum(cst + 32, sst + 64) - np.maximum(cst, sst), 0, None) / 16.0
    o1 = np.zeros((256, 65), f32)
    o1[:255, 0] = 1.0
    o1[:255, 1:] = ov
    c["c_ovl1"] = np.ascontiguousarray(o1.reshape(2, 128, 65).transpose(1, 0, 2))
    E = np.zeros((64, 32, 128), f32)
    for kt in range(32):
        E[2 * kt, kt, :64] = 1.0
        E[2 * kt + 1, kt, 64:] = 1.0
    c["c_E"] = E
    oh = np.zeros((2, 33, FLEN), f32)
    for idx in range(FLEN):
        n = idx - FOFF
        for var in range(2):
            masked = (n < 0) or (var == 1 and n >= 512)
            if masked:
                oh[var, 32, idx] = 1.0
            else:
                oh[var, _bucket(n), idx] += 1.0
                oh[var, 31, idx] -= 1.0
    c["c_oh"] = oh
    q = np.arange(128)[:, None]
    rel = np.arange(128)[None, :] - 64
    cur = (q >= 64).astype(np.int64)
    forced = (rel == cur) | (rel == cur - 1)
    fut = rel > cur
    c["c_keep"] = np.where(forced | fut, 0.0, 1.0).astype(f32)
    c["c_addm"] = np.where(forced, BIGV, np.where(fut, -BIGV, 0.0)).astype(f32)
    _CONSTS = c
    return c


CONST_SHAPES = {
    "c_ident": [128, 128], "c_anti": [128, 128], "c_cos": [128, NT, 64], "c_sin": [128, NT, 64],
    "c_decT": [128, 4, 128], "c_xi": [128, 4, 128], "c_zeta": [128, 4], "c_poolA": [128, 12, 128],
    "c_ovl1": [128, 2, 65], "c_E": [64, 32, 128], "c_oh": [2, 33, FLEN], "c_keep": [128, 128],
    "c_addm": [128, 128],
}

IN_SHAPES = {
    "x": [S, D], "mem": [256, D], "norm_g": [2, D], "final_g": [1, D], "mem_norm_g": [1, D],
    "rel_bias": [32, 12], "ev_w_in": [D, 5120], "ev_pool_w": [4, 192, 192], "ev_pool_scale": [1, 768],
    "ev_w_mem_kv": [D, 1024], "ev_w_out": [2048, D], "od_w_in": [D, 5668], "od_cmp_pe": [2, 32, 128],
    "od_cmp_w1": [2, 4096, 256], "od_cmp_b1": [2, 256], "od_cmp_w2": [2, 256, 128],
    "od_w_mem_kv": [D, 1024], "od_w_out": [2048, D],
}


def build_program(stage="full"):
    nc = bass.Bass("TRN2", target_bir_lowering=False)
    P = Prog(nc)
    C = make_consts()
    I = {k: P.dram(k, shp, F32, kind="ExternalInput") for k, shp in IN_SHAPES.items()}
    K = {k: P.dram(k, shp, F32, kind="ExternalInput") for k, shp in CONST_SHAPES.items()}
    out_d = P.dram("out", [S, D], F32, kind="ExternalOutput")
    h1_d = P.dram("h1", [S, D], F32, kind="ExternalOutput" if stage == "even" else "Internal")
    h1_tiles = [h1_d[i * 128:(i + 1) * 128, :].wb(P.buf(f"h1_{i}")) for i in range(NT)]

    banks = [P.psum(f"bank{i}", [128, 512], F32) for i in range(8)]
    rot_state = {"n": 0, "set": list(range(8))}

    def rot():
        s = rot_state["set"]
        b = banks[s[rot_state["n"] % len(s)]]
        rot_state["n"] += 1
        return b

    def bcast_rows(v, row, n):
        ap = bass.AP(v.ap.tensor, row * n, [[0, 128], [1, n]])
        return V(ap, v.bufs)

    with P.scope():
        ident = P.sbuf("ident", [128, 128], BF16)
        P.dma("pool", ident, K["c_ident"])
        eps_t = P.sbuf("eps_t", [128, 1], F32)
        P.memset("dve", eps_t, EPS)
        mkT = [P.sbuf(f"mkT{l}", [128, 4, 256], BF16) for l in range(2)]
        mv1 = [P.sbuf(f"mv1{l}", [128, 2, 4, 129], BF16) for l in range(2)]

        mhalf = P.sbuf("mhalf", [128, 4], F32)
        P.memset("pool", mhalf, -0.5)

        def rms_stats(ht, rstd, junk, ss, rt):
            P.act(junk, ht, AF.Square, accum_out=ss)
            P.ts("dve", rt, ss, 1.0 / D, ALU.mult, EPS, ALU.add)
            P.tt("pool", rstd, rt, mhalf[:, 0:1], ALU.pow)

        with P.scope():
            mg = P.sbuf("mg", [128, D], F32)
            P.dma("sp", mg, bcast_rows(I["mem_norm_g"], 0, D))
            junk = P.sbuf("junk0", [128, D], F32)
            memT = P.sbuf("memT", [128, 8, 256], BF16)
            for mc in range(2):
                mt = P.sbuf(f"mt{mc}", [128, D], F32)
                P.dma("sp", mt, I["mem"][mc * 128:(mc + 1) * 128, :])
                ss = P.sbuf(f"mss{mc}", [128, 1], F32)
                rt = P.sbuf(f"mrt{mc}", [128, 1], F32)
                rstd = P.sbuf(f"mrs{mc}", [128, 1], F32)
                rms_stats(mt, rstd, junk, ss, rt)
                mn = P.sbuf(f"mn{mc}", [128, D], BF16)
                P.stt(mn, mt, rstd, mg, ALU.mult, ALU.mult)
                bk = rot()
                bb = bk.bf()
                for k in range(8):
                    P.tr(bb[:, k * 128:(k + 1) * 128], mn[:, k * 128:(k + 1) * 128], ident)
                P.copy("dve", memT[:, :, mc * 128:(mc + 1) * 128],
                       bb.re("p (k t) -> p k t", k=8))
            for l, wname in enumerate(("ev_w_mem_kv", "od_w_mem_kv")):
                wkv = P.sbuf(f"wkv{l}", [128, 8, 1024], BF16)
                for k in range(8):
                    P.dma("pool", wkv[:, k, :], I[wname][k * 128:(k + 1) * 128, :], waw=False,
                          max_dma_last_dim=4096)
                for h in range(4):
                    bk = rot()
                    for k in range(8):
                        P.mm(bk[:, 0:256], wkv[:, k, h * 128:(h + 1) * 128], memT[:, k, :],
                             start=(k == 0), stop=(k == 7))
                    P.copy("act", mkT[l][:, h, :], bk[:, 0:256])
                P.memset("pool", mv1[l], 1.0)
                for mc in range(2):
                    bk = rot()
                    for k in range(8):
                        P.mm(bk[:, 0:512], memT[:, k, mc * 128:(mc + 1) * 128], wkv[:, k, 512:1024],
                             start=(k == 0), stop=(k == 7))
                    P.copy("dve", mv1[l][:, mc, :, 0:128], bk[:, 0:512].re("p (h d) -> p h d", h=4))

        def mem_attention(l, xqT_sb, y_dst, tag, pb, gate=None, between=None):
            pT = mem_pT[pb]
            bks = [rot(), rot()]
            for h in range(4):
                for mc in range(2):
                    ci = h * 2 + mc
                    P.mm(bks[ci // 4][:, (ci % 4) * 128:(ci % 4 + 1) * 128],
                         mkT[l][:, h, mc * 128:(mc + 1) * 128], xqT_sb[:, h, :])
            for j in range(2):
                P.act(pT[:, j * 4:(j + 1) * 4, :].re("p a t -> p (a t)"), bks[j], AF.Exp)
            if between is not None:
                between()
            oms = [rot(), rot()]
            for h in range(4):
                ob = oms[h // 2][:, (h % 2) * 129:(h % 2) * 129 + 129]
                for mc in range(2):
                    P.mm(ob, pT[:, h * 2 + mc, :], mv1[l][:, mc, h, :],
                         start=(mc == 0 and h % 2 == 0), stop=(mc == 1), skip_group_check=True)
            rs = mem_rs[pb]
            for h in range(4):
                ob = oms[h // 2]
                P.recip(rs[:, h:h + 1], ob[:, (h % 2) * 129 + 128:(h % 2) * 129 + 129])
            for h in range(4):
                ob = oms[h // 2]
                if gate is None:
                    P.ts("dve", y_dst[:, h * 128:(h + 1) * 128], ob[:, (h % 2) * 129:(h % 2) * 129 + 128],
                         rs[:, h:h + 1], ALU.mult)
                else:
                    P.stt(y_dst[:, h * 128:(h + 1) * 128], ob[:, (h % 2) * 129:(h % 2) * 129 + 128],
                          rs[:, h:h + 1], gate[:, h * 128:(h + 1) * 128], ALU.mult, ALU.mult)

        def out_proj(y_bf, w_out, h_in, h_out, pb, scale=0.5):
            yT = yT_t[pb]
            for half in range(2):
                bk = rot()
                bb = bk.bf()
                for k in range(8):
                    kk = half * 8 + k
                    P.tr(bb[:, k * 128:(k + 1) * 128], y_bf[:, kk * 128:(kk + 1) * 128], ident)
                P.copy("act" if half == 0 else "dve", yT[:, half * 8:(half + 1) * 8, :].re("p k t -> p (k t)"), bb)
            for n in range(2):
                bk = rot()
                for k in range(16):
                    P.mm(bk, yT[:, k, :], w_out[:, k, n * 512:(n + 1) * 512], start=(k == 0), stop=(k == 15))
                P.stt(h_out[:, n * 512:(n + 1) * 512], bk, scale, h_in[:, n * 512:(n + 1) * 512], ALU.mult, ALU.add)

        with P.scope():
            w_in = P.sbuf("ev_w_in_sb", [128, 8, 5120], BF16)
            for k in range(8):
                P.dma("pool", w_in[:, k, :], I["ev_w_in"][k * 128:(k + 1) * 128, :],
                      waw=False, max_dma_last_dim=4096)
            w_out = P.sbuf("ev_w_out_sb", [128, 16, 1024], BF16)
            for k0 in range(0, 16, 4):
                P.dma("pool", w_out[:, k0:k0 + 4, :],
                      I["ev_w_out"][k0 * 128:(k0 + 4) * 128, :].re("(k p) c -> p k c", p=128), waw=False,
                      max_dma_last_dim=4096)
            g_ev = P.sbuf("g_ev", [128, D], F32)
            P.dma("sp", g_ev, bcast_rows(I["norm_g"], 0, D))
            poolw = P.sbuf("poolw", [96, 4, 2, 192], BF16)
            with P.scope():
                pscale = P.sbuf("pscale", [128, 768], F32)
                P.dma("sp", pscale, bcast_rows(I["ev_pool_scale"], 0, 768))
                poolw0 = P.sbuf("poolw0", [96, 4, 2, 192], F32)
                for g in range(4):
                    P.dma("sp", poolw0[:, g, :, :], I["ev_pool_w"][g].re("(cc p) d -> p cc d", p=96), waw=False)
                for g in range(4):
                    for cc in range(2):
                        P.tt("dve", poolw[:, g, cc, :], poolw0[:, g, cc, :], pscale[0:96, g * 192:(g + 1) * 192], ALU.mult)
            poolA = P.sbuf("poolA", [128, 12, 128], BF16)
            P.dma("pool", poolA, K["c_poolA"])
            cos_t = [P.sbuf(f"cos_t{j}", [128, 1, 64], F32) for j in range(2)]
            sin_t = [P.sbuf(f"sin_t{j}", [128, 1, 64], F32) for j in range(2)]
            decT = P.sbuf("decT", [128, 4, 128], BF16)
            xi_t = P.sbuf("xi_t", [128, 4, 128], BF16)
            zeta = P.sbuf("zeta", [128, 4], F32)
            P.dma("pool", decT, K["c_decT"])
            P.dma("pool", xi_t, K["c_xi"])
            P.dma("sp", zeta, K["c_zeta"])
            Rst = P.sbuf("Rst", [128, 4, 192], F32)
            Rbf = P.sbuf("Rbf", [128, 4, 192], BF16)
            P.memset("dve", Rst, 0.0)
            P.memset("dve", Rbf, 0.0)
            gch = C["gchunk"]

            print("sbuf remaining before even work tiles", nc.sbuf_bytes_remaining)
            hT = [P.sbuf(f"hT{j}", [128, D], F32) for j in range(3)]
            u_bf = P.sbuf("u_bf", [128, D], BF16)
            junk = u_bf
            th_t = P.sbuf("th_t", [128, 512], F32)
            ss = [P.sbuf(f"ss{j}", [128, 1], F32) for j in range(2)]
            rtt = [P.sbuf(f"rtt{j}", [128, 1], F32) for j in range(2)]
            rstd = [P.sbuf(f"rstd{j}", [128, 1], F32) for j in range(2)]
            uTt = P.sbuf("uT", [128, 8, 128], BF16)
            za = [P.sbuf(f"za{j}", [128, 768], BF16) for j in range(3)]
            qk_f = P.sbuf("qk_f", [128, 1024], F32)
            rt1 = P.sbuf("rt1", [128, 8, 64], BF16)
            rt2 = P.sbuf("rt2", [128, 8, 64], BF16)
            rt3 = P.sbuf("rt3", [128, 8, 64], BF16)
            rt4 = P.sbuf("rt4", [128, 8, 64], BF16)
            qk_rot = P.sbuf("qk_rot", [128, 8, 2, 64], BF16)
            kz = [P.sbuf(f"kz{j}", [128, 4, 128], BF16) for j in range(2)]
            qT = [P.sbuf(f"qT{j}", [128, 4, 128], BF16) for j in range(2)]
            qxT = [P.sbuf(f"qxT{j}", [128, 4, 128], BF16) for j in range(2)]
            kT = [P.sbuf(f"kT{j}", [128, 4, 128], BF16) for j in range(2)]
            v_bf = [P.sbuf(f"v_bf{j}", [128, 768], BF16) for j in range(2)]
            sg = [P.sbuf(f"sg{j}", [128, 2048], BF16) for j in range(2)]
            xqT_sb = [P.sbuf(f"xqT_sb{j}", [128, 4, 128], BF16) for j in range(2)]
            att = P.sbuf("att", [128, 4, 128], BF16)
            bst = P.sbuf("bst", [128, 4, 6], F32)
            mv = P.sbuf("mv", [128, 4, 2], F32)
            grt = P.sbuf("grt", [128, 4], F32)
            grs = P.sbuf("grs", [128, 4], F32)
            tmpR = P.sbuf("tmpR", [128, 4, 192], F32)
            y_bf = P.sbuf("y_bf", [128, 2048], BF16)
            ybufs = [P.buf(f"ybuf{j}") for j in range(7)]
            tbufs = [P.buf(f"tbuf{j}") for j in range(4)]
            y_bf_all = V(y_bf.ap, ybufs)
            pooledT = P.sbuf("pooledT", [96, 8, 128], BF16)
            mem_pT = [P.sbuf("mem_pT", [128, 8, 128], BF16)] * 2
            mem_rs = [P.sbuf("mem_rs", [128, 4], F32)] * 2
            yT_t = [P.sbuf("yT", [128, 16, 128], BF16)] * 2
            print("sbuf remaining after even work tiles", nc.sbuf_bytes_remaining)

            def ev_load(i):
                P.dma("sp", hT[i % 3], I["x"][i * 128:(i + 1) * 128, :])
                P.dma("sp", cos_t[i % 2], K["c_cos"][:, i:i + 1, :])
                P.dma("sp", sin_t[i % 2], K["c_sin"][:, i:i + 1, :])

            ev_load(0)

            def ev_head(i):
                rms_stats(hT[i % 3], rstd[i % 2], junk, ss[i % 2], rtt[i % 2])
                P.stt(u_bf, hT[i % 3], rstd[i % 2], g_ev, ALU.mult, ALU.mult)

            ev_head(0)

            def ev_stage_a(i):
                pb = i % 2
                ht = hT[i % 3]
                if i + 1 < NT:
                    ev_load(i + 1)
                bk = rot()
                bb = bk.bf()
                for k in range(8):
                    P.tr(bb[:, k * 128:(k + 1) * 128], u_bf[:, k * 128:(k + 1) * 128], ident)
                P.copy("act", uTt.re("p k t -> p (k t)"), bb)

                def proj_tok(c0, width):
                    b = rot()
                    for k in range(8):
                        P.mm(b[:, 0:width], uTt[:, k, :], w_in[:, k, c0:c0 + width], start=(k == 0), stop=(k == 7))
                    return b

                b = proj_tok(768, 512)
                P.copy("act", qk_f[:, 0:512], b)
                b = proj_tok(1280, 512)
                P.copy("dve", qk_f[:, 512:1024], b)
                qv = qk_f.re("p (h two j) -> p h two j", h=8, two=2)
                x1 = qv[:, :, 0, :]
                x2 = qv[:, :, 1, :]
                cb = cos_t[pb].bcast([128, 8, 64])
                sb_ = sin_t[pb].bcast([128, 8, 64])
                P.tt("dve", rt1, x1, cb, ALU.mult)
                P.tt("pool", rt2, x2, sb_, ALU.mult)
                P.tt("pool", rt4, x2, cb, ALU.mult)
                P.tt("dve", rt3, x1, sb_, ALU.mult)
                P.tt("dve", qk_rot[:, :, 0, :], rt1, rt2, ALU.subtract)
                P.tt("dve", qk_rot[:, :, 1, :], rt3, rt4, ALU.add)
                qkr = qk_rot.re("p h two j -> p (h two j)")
                P.tt("pool", kz[pb], qkr[:, 512:1024].re("p (h d) -> p h d", h=4),
                     zeta[:, :].re("p (h o) -> p h o", o=1).bcast([128, 4, 128]), ALU.mult)
                b = proj_tok(0, 512)
                P.copy("act", za[i % 3][:, 0:512], b)
                b = proj_tok(512, 256)
                P.copy("dve", za[i % 3][:, 512:768], b[:, 0:256])
                b = proj_tok(1792, 512)
                P.copy("act", v_bf[pb][:, 0:512], b)
                b = proj_tok(2304, 256)
                P.copy("dve", v_bf[pb][:, 512:768], b[:, 0:256])
                if i + 1 < NT:
                    ev_head(i + 1)
                for j in range(4):
                    b = proj_tok(3072 + 512 * j, 512)
                    P.act(th_t, b, AF.Tanh, scale=0.5)
                    P.stt(sg[pb][:, j * 512:(j + 1) * 512], th_t, 1.0, b, ALU.add, ALU.mult)
                b = rot()
                for h in range(4):
                    for k in range(8):
                        P.mm(b[:, h * 128:(h + 1) * 128], w_in[:, k, 2560 + h * 128:2560 + (h + 1) * 128], uTt[:, k, :],
                             start=(k == 0), stop=(k == 7))
                P.ts("dve", xqT_sb[pb].re("p h t -> p (h t)"), b, 128.0 ** -0.5, ALU.mult)
                bk = rot()
                bb = bk.bf()
                for j in range(8):
                    P.tr(bb[:, j * 128:(j + 1) * 128], qkr[:, j * 128:(j + 1) * 128], ident)
                P.copy("act", qT[pb].re("p h t -> p (h t)"), bb[:, 0:512])
                P.tt("dve", qxT[pb].re("p h t -> p (h t)"), bb[:, 0:512], xi_t.re("p h t -> p (h t)"), ALU.mult)
                P.copy("act", kT[pb].re("p h t -> p (h t)"), bb[:, 512:1024])

            def ev_stage_b(i):
                pb = i % 2
                ht = hT[i % 3]
                zc = za[i % 3]
                zp = za[(i - 1) % 3]
                bk = rot()
                for h in range(4):
                    P.mm(bk[:, h * 128:(h + 1) * 128], kT[pb][:, h, :], qT[pb][:, h, :])
                P.tt("dve", att.re("p h t -> p (h t)"), bk, decT.re("p h t -> p (h t)"), ALU.mult)
                ppb = [rot(), rot()]
                for g in range(4):
                    for cc in range(2):
                        ci = g * 2 + cc
                        dst = ppb[ci // 4][0:96, (ci % 4) * 128:(ci % 4 + 1) * 128]
                        c0 = g * 192 + cc * 96
                        if i == 0:
                            P.mm(dst, zc[:, c0:c0 + 96], poolA[:, g * 3 + 2, :])
                        else:
                            P.mm(dst, zp[:, c0:c0 + 96], poolA[:, g * 3 + 1, :], start=True, stop=False)
                            P.mm(dst, zc[:, c0:c0 + 96], poolA[:, g * 3 + 0, :], start=False, stop=True)
                for j in range(2):
                    P.copy("act", pooledT[:, j * 4:(j + 1) * 4, :].re("p a t -> p (a t)"), ppb[j][0:96, :])
                obk = [banks[0], banks[1]]
                kvb = [banks[2], banks[3]]

                def between():
                    for h in range(4):
                        ob = obk[h // 2][:, (h % 2) * 192:(h % 2) * 192 + 192]
                        P.mm(ob, att[:, h, :], v_bf[pb][:, h * 192:(h + 1) * 192], start=True, stop=False)
                        P.mm(ob, qxT[pb][:, h, :], Rbf[:, h, :], start=False, stop=True)
                    for h in range(4):
                        kb = kvb[h // 2][:, (h % 2) * 192:(h % 2) * 192 + 192]
                        P.mm(kb, kz[pb][:, h, :], v_bf[pb][:, h * 192:(h + 1) * 192])
                    ypb = [rot(), rot()]
                    for g in range(4):
                        yb = ypb[g // 2][:, (g % 2) * 192:(g % 2) * 192 + 192]
                        for cc in range(2):
                            P.mm(yb, pooledT[:, g * 2 + cc, :], poolw[:, g, cc, :], start=(cc == 0), stop=(cc == 1))
                    for h in range(4):
                        ob = obk[h // 2][:, (h % 2) * 192:(h % 2) * 192 + 192]
                        P.bn_stats(bst[:, h, :], ob)
                    for h in range(4):
                        P.bn_aggr(mv[:, h, :], bst[:, h, :])
                    P.ts("dve", grt, mv[:, :, 1], EPS, ALU.add)
                    P.tt("pool", grs, grt, mhalf, ALU.pow)
                    for j in range(2):
                        P.tt("dve", y_bf[:, j * 384:(j + 1) * 384].wb(ybufs[j]), ypb[j][:, 0:384],
                             sg[pb][:, j * 384:(j + 1) * 384], ALU.mult)
                    for h in range(4):
                        ob = obk[h // 2][:, (h % 2) * 192:(h % 2) * 192 + 192]
                        tr_ = tmpR[:, h, :].wb(tbufs[h])
                        P.stt(tr_, ob, mv[:, h, 0:1], sg[pb][:, 768 + h * 192:768 + (h + 1) * 192],
                              ALU.subtract, ALU.mult)
                        P.act(y_bf[:, 768 + h * 192:768 + (h + 1) * 192].wb(ybufs[2 + h]), tr_, AF.Copy,
                              scale=grs[:, h:h + 1])
                mem_attention(0, xqT_sb[pb], y_bf[:, 1536:2048].wb(ybufs[6]), "ev", pb, gate=sg[pb][:, 1536:2048], between=between)
                for h in range(4):
                    kb = kvb[h // 2][:, (h % 2) * 192:(h % 2) * 192 + 192]
                    P.stt(Rst[:, h, :], Rst[:, h, :], gch[h], kb, ALU.mult, ALU.add)
                P.copy("pool", Rbf, Rst)
                out_proj(y_bf_all, w_out, ht, ht, pb)
                P.dma("sp", h1_tiles[i], ht)

            rot_state["set"] = [4, 5, 6, 7]
            rot_state["n"] = 0
            for step in range(NT + 1):
                if step < NT:
                    ev_stage_a(step)
                if step >= 1:
                    ev_stage_b(step - 1)
            rot_state["set"] = list(range(8))

        if stage == "even":
            P.emit()
            return nc

        def dtiles(name, per_shape, dtype):
            t = P.dram(name, [NT] + list(per_shape), dtype)
            return [t[i].wb(P.buf(f"{name}_{i}")) for i in range(NT)]

        qT_d = dtiles("qT_d", [128, 12 * 128], BF16)
        gates_d = dtiles("gates_d", [128, 36], F32)
        sg_d = dtiles("sg_d", [128, 2048], BF16)
        ym_d = dtiles("ym_d", [128, 512], F32)
        kwT_d = dtiles("kwT_d", [128, 256], BF16)
        vw1_d = dtiles("vw1_d", [128, 2 * 129], BF16)
        F_d = P.dram("F_d", [2, 12, FLEN], BF16)

        with P.scope():
            ksT_all = P.sbuf("ksT_all", [128, 2, S], BF16)
            ksT_tiles = [ksT_all[:, :, i * 128:(i + 1) * 128].wb(P.buf(f"ksT_{i}")) for i in range(NT)]
            vs1_all = P.sbuf("vs1_all", [128, NT, 2, 129], BF16)
            P.memset("pool", vs1_all, 1.0)
            kcmpT = P.sbuf("kcmpT", [128, 2, 256], BF16)
            P.memset("pool", kcmpT, 0.0)
            vc1 = P.sbuf("vc1", [128, 2, 2, 193], BF16)
            P.memset("pool", vc1, 0.0)

            with P.scope():
                kvcT = P.sbuf("kvcT", [128, 4, S], BF16)
                kvc_bufs = [P.buf(f"kvc_{i}") for i in range(NT)]
                with P.scope():
                    w_in = P.sbuf("od_w_in_sb", [128, 8, 5668], BF16)
                    for k in range(8):
                        P.dma("pool", w_in[:, k, :], I["od_w_in"][k * 128:(k + 1) * 128, :],
                              waw=False, max_dma_last_dim=4096)
                    g_od = P.sbuf("g_od", [128, D], F32)
                    P.dma("sp", g_od, bcast_rows(I["norm_g"], 1, D))
                    print("sbuf remaining before passA work tiles", nc.sbuf_bytes_remaining)
                    hT = [P.sbuf(f"ahT{j}", [128, D], F32) for j in range(2)]
                    ath_t = P.sbuf("ath_t", [128, 512], F32)
                    ss = [P.sbuf(f"ass{j}", [128, 1], F32) for j in range(2)]
                    rtt = [P.sbuf(f"artt{j}", [128, 1], F32) for j in range(2)]
                    rstd = [P.sbuf(f"arstd{j}", [128, 1], F32) for j in range(2)]
                    u_bf = P.sbuf("au_bf", [128, D], BF16)
                    junk = u_bf
                    uTt = P.sbuf("auT", [128, 8, 128], BF16)
                    qT_sb = [P.sbuf(f"aqT{j}", [128, 12, 128], BF16) for j in range(2)]
                    kwT_sb = [P.sbuf(f"akwT{j}", [128, 2, 128], BF16) for j in range(2)]
                    vw1_sb = [P.sbuf(f"avw1{j}", [128, 2, 129], BF16) for j in range(2)]
                    for j in range(2):
                        P.memset("pool", vw1_sb[j], 1.0)
                    xqT_sb = [P.sbuf(f"axqT{j}", [128, 4, 128], BF16) for j in range(2)]
                    gates_sb = [P.sbuf(f"agates{j}", [128, 36], F32) for j in range(2)]
                    sg_sb = [P.sbuf(f"asg{j}", [128, 2048], BF16) for j in range(2)]
                    ym_sb = [P.sbuf(f"aym{j}", [128, 512], F32) for j in range(2)]
                    mem_pT = [P.sbuf("amem_pT", [128, 8, 128], BF16)] * 2
                    mem_rs = [P.sbuf("amem_rs", [128, 4], F32)] * 2

                    QS = 128.0 ** -0.5

                    P.dma("sp", hT[0], h1_tiles[0])

                    def od_head(i):
                        rms_stats(hT[i % 2], rstd[i % 2], junk, ss[i % 2], rtt[i % 2])
                        P.stt(u_bf, hT[i % 2], rstd[i % 2], g_od, ALU.mult, ALU.mult)

                    od_head(0)

                    def od_stage_a(i):
                        pb = i % 2
                        ht = hT[pb]
                        if i + 1 < NT:
                            P.dma("sp", hT[(i + 1) % 2], h1_tiles[i + 1])
                        bk = rot()
                        bb = bk.bf()
                        for k in range(8):
                            P.tr(bb[:, k * 128:(k + 1) * 128], u_bf[:, k * 128:(k + 1) * 128], ident)
                        P.copy("act", uTt.re("p k t -> p (k t)"), bb)

                        def fm_group(cols):
                            b = rot()
                            for j, c0 in enumerate(cols):
                                for k in range(8):
                                    P.mm(b[:, j * 128:(j + 1) * 128], w_in[:, k, c0:c0 + 128], uTt[:, k, :],
                                         start=(k == 0), stop=(k == 7))
                            return b

                        for qg in range(3):
                            b = fm_group([(qg * 4 + j) * 128 for j in range(4)])
                            if qg == 1:
                                P.ts("dve", qT_sb[pb][:, qg * 4:(qg + 1) * 4, :].re("p h t -> p (h t)"), b, QS, ALU.mult)
                            else:
                                P.act(qT_sb[pb][:, qg * 4:(qg + 1) * 4, :].re("p h t -> p (h t)"), b, AF.Copy, scale=QS)
                        b = fm_group([2048, 2176, 2560, 2688])
                        P.copy("dve", ksT_tiles[i], b[:, 0:256].re("p (g t) -> p g t", g=2))
                        P.copy("act", kwT_sb[pb].re("p g t -> p (g t)"), b[:, 256:512])
                        b = fm_group([1536, 1664, 1792, 1920])
                        P.copy("dve", kvcT[:, :, i * 128:(i + 1) * 128].wb(kvc_bufs[i]),
                               b.re("p (c t) -> p c t", c=4))
                        b = fm_group([3108 + h * 128 for h in range(4)])
                        P.ts("dve", xqT_sb[pb].re("p h t -> p (h t)"), b, QS, ALU.mult)

                        def proj_tok(b, o0, c0, width, first=True):
                            for k in range(8):
                                P.mm(b[:, o0:o0 + width], uTt[:, k, :], w_in[:, k, c0:c0 + width],
                                     start=(k == 0), stop=(k == 7))

                        b = rot()
                        proj_tok(b, 0, 2304, 256)
                        proj_tok(b, 256, 2816, 256)
                        P.copy("act", vs1_all[:, i, :, 0:128], b[:, 0:256].re("p (g d) -> p g d", g=2))
                        P.copy("dve", vw1_sb[pb][:, :, 0:128], b[:, 256:512].re("p (g d) -> p g d", g=2))
                        if i + 1 < NT:
                            od_head(i + 1)
                        b = rot()
                        proj_tok(b, 0, 3072, 36)
                        P.act(gates_sb[pb], b[:, 0:36], AF.Tanh, scale=0.5)
                        P.ts("dve", gates_sb[pb], gates_sb[pb], 0.5, ALU.mult, 0.5, ALU.add)
                        for j in range(4):
                            b = rot()
                            proj_tok(b, 0, 3620 + 512 * j, 512)
                            P.act(ath_t, b, AF.Tanh, scale=0.5)
                            P.stt(sg_sb[pb][:, j * 512:(j + 1) * 512], ath_t, 1.0, b, ALU.add, ALU.mult)
                        P.dma("sp", qT_d[i], qT_sb[pb].re("p h t -> p (h t)"))
                        P.dma("sp", kwT_d[i], kwT_sb[pb].re("p g t -> p (g t)"))
                        P.dma("sp", vw1_d[i], vw1_sb[pb].re("p g t -> p (g t)"))
                        P.dma("sp", gates_d[i], gates_sb[pb])
                        P.dma("sp", sg_d[i], sg_sb[pb])

                    def od_stage_b(i):
                        pb = i % 2
                        mem_attention(1, xqT_sb[pb], ym_sb[pb], "od", pb)
                        P.dma("sp", ym_d[i], ym_sb[pb])

                    for step in range(NT + 1):
                        if step < NT:
                            od_stage_a(step)
                        if step >= 1:
                            od_stage_b(step - 1)

                with P.scope():
                    kvc_all = V(kvcT.ap, kvc_bufs)
                    w1 = P.sbuf("cw1", [128, 2, 32, 256], BF16)
                    for t in range(2):
                        for l0 in range(0, 32, 8):
                            P.dma("pool", w1[:, t, l0:l0 + 8, :],
                                  I["od_cmp_w1"][t, l0 * 128:(l0 + 8) * 128, :].re("(l p) h -> p l h", p=128),
                                  waw=False)
                    w2 = P.sbuf("cw2", [128, 2, 2, 128], BF16)
                    for t in range(2):
                        P.dma("pool", w2[:, t, :, :], I["od_cmp_w2"][t].re("(hc p) d -> p hc d", p=128), waw=False)
                    pe_sb = P.sbuf("pe_sb", [32, 2, 128], F32)
                    for t in range(2):
                        P.dma("sp", pe_sb[:, t, :], I["od_cmp_pe"][t], waw=False)
                    identf = P.sbuf("identf", [32, 32], F32)
                    P.dma("sp", identf, K["c_ident"][0:32, 0:32])
                    peT = P.sbuf("peT", [128, 2, 32], BF16)
                    for t in range(2):
                        bk = rot()
                        P.tr(bk[:, 0:32], pe_sb[:, t, :], identf)
                        P.copy("dve", peT[:, t, :], bk[:, 0:32])
                    b1col = P.sbuf("b1col", [128, 4], F32)
                    P.dma("sp", b1col, I["od_cmp_b1"].re("t (hc p) -> p (t hc)", p=128), allow_slow_non_contiguous=True)
                    bk = rot()
                    for t in range(2):
                        for hc in range(2):
                            col = t * 2 + hc
                            for l in range(32):
                                P.mm(bk[:, col:col + 1], w1[:, t, l, hc * 128:(hc + 1) * 128], peT[:, t, l:l + 1],
                                     start=(l == 0), stop=(l == 31))
                    b1p = P.sbuf("b1p", [128, 4], F32)
                    P.tt("dve", b1p, bk[:, 0:4], b1col, ALU.add)
                    for g in range(2):
                        P.dma("pool", vc1[:, :, g, 128:193], K["c_ovl1"])
                    h1T = [P.sbuf(f"h1T{j}", [128, 2, 256], BF16) for j in range(2)]
                    for t in range(2):
                        for g in range(2):
                            hb = rot()
                            for hc in range(2):
                                for l in range(32):
                                    P.mm(hb[:, hc * 256:hc * 256 + 255], w1[:, t, l, hc * 128:(hc + 1) * 128],
                                         kvc_all[:, t * 2 + g, l:l + 16 * 254 + 1:16], start=(l == 0), stop=(l == 31))
                            hT_ = h1T[g]
                            for hc in range(2):
                                P.act(hT_[:, hc, 0:255], hb[:, hc * 256:hc * 256 + 255], AF.Silu,
                                      bias=b1p[:, t * 2 + hc:t * 2 + hc + 1])
                            if t == 0:
                                kb = rot()
                                for hc in range(2):
                                    P.mm(kb[:, 0:255], w2[:, 0, hc, :], hT_[:, hc, 0:255], start=(hc == 0), stop=(hc == 1))
                                P.copy("dve", kcmpT[:, g, 0:255], kb[:, 0:255])
                            else:
                                vb = rot()
                                for jt in range(2):
                                    nj = 128 if jt == 0 else 127
                                    for hc in range(2):
                                        P.mm(vb[0:nj, jt * 128:(jt + 1) * 128], hT_[:, hc, jt * 128:jt * 128 + nj],
                                             w2[:, 1, hc, :], start=(hc == 0), stop=(hc == 1))
                                    P.copy("dve", vc1[0:nj, jt, g, 0:128], vb[0:nj, jt * 128:(jt + 1) * 128])

            with P.scope():
                with P.scope():
                    tabx = P.sbuf("tabx", [33, 12], F32)
                    P.memset("dve", tabx[32:33, :], NEG)
                    P.dma("sp", tabx[0:32, :], I["rel_bias"], waw=False)
                    oh_sb = P.sbuf("oh_sb", [33, 2, FLEN], F32)
                    for var in range(2):
                        P.dma("sp", oh_sb[:, var, :], K["c_oh"][var], waw=False)
                    F_sb = P.sbuf("F_sb", [12, 2, FLEN], BF16)
                    for var in range(2):
                        for c in range(FLEN // 512):
                            bk = rot()
                            P.mm(bk[0:12, :], tabx, oh_sb[:, var, c * 512:(c + 1) * 512])
                            P.copy("dve" if c % 2 else "act", F_sb[:, var, c * 512:(c + 1) * 512], bk[0:12, :])
                    P.dma("sp", F_d.re("v h n -> h v n"), F_sb)
                rot_state["set"] = [4, 5, 6, 7]
                rot_state["n"] = 0
                ACC = banks[0:4]
                w_out = P.sbuf("od_w_out_sb", [128, 16, 1024], BF16)
                for k0 in range(0, 16, 4):
                    P.dma("pool", w_out[:, k0:k0 + 4, :],
                          I["od_w_out"][k0 * 128:(k0 + 4) * 128, :].re("(k p) c -> p k c", p=128), waw=False,
                          max_dma_last_dim=4096)
                anti = P.sbuf("anti", [128, 128], BF16)
                P.dma("pool", anti, K["c_anti"])
                E_sb = P.sbuf("E_sb", [64, 32, 128], BF16)
                for k0 in range(0, 32, 8):
                    P.dma("pool", E_sb[:, k0:k0 + 8, :], K["c_E"][:, k0:k0 + 8, :], waw=False, max_dma_last_dim=4096)
                keep_t = P.sbuf("keep_t", [128, 128], F32)
                addm_t = P.sbuf("addm_t", [128, 128], F32)
                P.dma("sp", keep_t, K["c_keep"])
                P.dma("sp", addm_t, K["c_addm"])
                g_fin = P.sbuf("g_fin", [128, D], F32)
                P.dma("sp", g_fin, bcast_rows(I["final_g"], 0, D))

                def hankel(var, off, pstep):
                    ap = bass.AP(F_d.ap.tensor, var * 12 * FLEN + off, [[pstep, 128], [FLEN, 12], [1, 128]])
                    return V(ap, F_d.bufs)

                Tb = {}
                for dl, var in ((0, 0), (128, 0), (512, 1)):
                    Tb[dl] = P.sbuf(f"Tb{dl}", [128, 12, 128], BF16)
                    P.dma("sp", Tb[dl], hankel(var, FOFF + dl - 127, 1))
                cbias = [[P.sbuf(f"cbias{j}_{jt}", [128, 12, 128], BF16) for jt in range(2)] for j in range(2)]
                kw_ring = P.sbuf("kw_ring", [128, 6, 2, 128], BF16)
                vw_ring = P.sbuf("vw_ring", [128, 6, 2, 129], BF16)
                kw_slots = [kw_ring[:, s_, :, :].wb(P.buf(f"kwslot{s_}")) for s_ in range(6)]
                vw_slots = [vw_ring[:, s_, :, :].wb(P.buf(f"vwslot{s_}")) for s_ in range(6)]
                print("sbuf remaining before passB work tiles", nc.sbuf_bytes_remaining)
                hT = [P.sbuf(f"bhT{j}", [128, D], F32) for j in range(2)]
                junk = P.sbuf("bjunk", [128, D], BF16)
                ss = [P.sbuf(f"bss{j}", [128, 1], F32) for j in range(2)]
                rtt = [P.sbuf(f"brtt{j}", [128, 1], F32) for j in range(2)]
                rstd = [P.sbuf(f"brstd{j}", [128, 1], F32) for j in range(2)]
                qT_sb = [P.sbuf(f"bqT{j}", [128, 12, 128], BF16) for j in range(2)]
                gates_sb = [P.sbuf(f"bgates{j}", [128, 36], F32) for j in range(2)]
                sg_sb = [P.sbuf(f"bsg{j}", [128, 2048], BF16) for j in range(2)]
                y_pre = [P.sbuf(f"by_pre{j}", [128, 2048], F32) for j in range(2)]
                y_bf = [P.sbuf(f"by_bf{j}", [128, 2048], BF16) for j in range(2)]
                ybB = [[P.buf(f"ybB{j}_{k}") for k in range(4)] for j in range(2)]
                yT_t = [P.sbuf("byT", [128, 16, 128], BF16)] * 2
                out_t = [P.sbuf(f"bout{j}", [128, D], F32) for j in range(2)]
                pTs = [P.sbuf(f"bpT{j}", [128, 3, 128], BF16) for j in range(4)]
                pt_state = {"n": 0}

                def next_pT():
                    t_ = pTs[pt_state["n"] % 4]
                    pt_state["n"] += 1
                    return t_

                zt = [P.sbuf(f"bzt{j}", [128, 6], F32) for j in range(2)]
                rz = [P.sbuf(f"brz{j}", [128, 6], F32) for j in range(2)]
                aco = [P.sbuf(f"baco{j}", [128, 6], F32) for j in range(2)]
                imp = [P.sbuf(f"bimp{j}", [128, 64], F32) for j in range(2)]
                imp2 = [P.sbuf(f"bimp2{j}", [128, 64], F32) for j in range(2)]
                m8 = [P.sbuf(f"bm8{j}", [128, 8], F32) for j in range(2)]
                negsel = [P.sbuf(f"bnegsel{j}", [128, 64], BF16) for j in range(2)]
                selT = [P.sbuf(f"bselT{j}", [64, 128], BF16) for j in range(2)]
                maskT = [P.sbuf(f"bmaskT{j}", [128, NT, 128], BF16) for j in range(2)]

                def branch_scales(accs, ncols, nper, gate_cols, gi):
                    for bi in range(6 // nper):
                        zsrc = accs[bi][:, 128:128 + ncols * (nper - 1) + 1:ncols]
                        P.ts("dve", zt[gi][:, bi * nper:(bi + 1) * nper], zsrc, 1e-30, ALU.add)
                    P.recip(rz[gi], zt[gi])
                    P.tt("dve", aco[gi], rz[gi], gate_cols, ALU.mult)

                NPT = 9
                pTs2 = [P.sbuf(f"bpTx{j}", [128, 3, 128], BF16) for j in range(NPT)]
                pipe_q = []
                LOOK = 7

                def pipe_push(fn):
                    pipe_q.append(fn)
                    while len(pipe_q) > LOOK:
                        pipe_q.pop(0)()

                def pipe_flush():
                    while pipe_q:
                        pipe_q.pop(0)()

                def next_pT2():
                    t_ = pTs2[pt_state["n"] % NPT]
                    pt_state["n"] += 1
                    return t_

                def unit(lhsT_k, q3, extra, accs_dst, rhs_v, starts, stop, mask=None):
                    sbk = rot()
                    n_e = len(extra)
                    P.mm(sbk[:, 0:384], lhsT_k, q3, start=True, stop=(n_e == 0), skip_group_check=True)
                    for j, (l_, r_) in enumerate(extra):
                        P.mm(sbk[:, 0:384], l_, r_, start=False, stop=(j == n_e - 1), skip_group_check=True)
                    pT = next_pT2()
                    P.act(pT.re("p h t -> p (h t)"), sbk[:, 0:384], AF.Exp)
                    if mask is not None:
                        P.tt("dve", pT, pT, mask, ALU.mult)

                    def s2():
                        for hd3 in range(3):
                            P.mm(accs_dst[hd3], pT[:, hd3, :], rhs_v, start=starts[hd3], stop=stop,
                                 skip_group_check=True)
                    pipe_push(s2)

                for i in range(NT):
                    pb = i % 2
                    ht = hT[pb]
                    P.dma("sp", ht, h1_tiles[i])
                    P.dma("sp", qT_sb[pb].re("p h t -> p (h t)"), qT_d[i])
                    P.dma("sp", gates_sb[pb], gates_d[i])
                    P.dma("sp", sg_sb[pb], sg_d[i])
                    P.dma("sp", y_pre[pb][:, 1536:2048], ym_d[i])
                    P.tt("pool", y_bf[pb][:, 1536:2048].wb(ybB[pb][3]), y_pre[pb][:, 1536:2048],
                         sg_sb[pb][:, 1536:2048], ALU.mult)
                    P.dma("sp", kw_slots[i % 6].re("p g t -> p (g t)"), kwT_d[i])
                    P.dma("sp", vw_slots[i % 6].re("p g t -> p (g t)"), vw1_d[i])
                    jts = [0] + ([1] if i >= 16 else [])
                    need_cb = {}
                    for jt in jts:
                        need_cb[jt] = (jt == 1) or (i <= 17)
                        if need_cb[jt]:
                            P.dma("sp", cbias[pb][jt], hankel(0, FOFF + 128 * i - 2048 * jt - 31 - 2032, 16))
                    qf = qT_sb[pb]
                    for g in range(2):
                        gi = g

                        def q3of(hh, g=g, qf=qf):
                            return qf[:, g * 6 + hh * 3:g * 6 + hh * 3 + 3, :].re("p h t -> p (h t)")

                        def hsl(tile_, hh, g=g):
                            return tile_[:, g * 6 + hh * 3:g * 6 + hh * 3 + 3, :].re("p h t -> p (h t)")

                        CB = [2, 3, 0] if g == 0 else [0, 1, 2]
                        cacc = [ACC[b_] for b_ in CB]
                        early = [hd for hd in range(6) if CB[hd // 2] in (2, 3)]
                        late = [hd for hd in range(6) if CB[hd // 2] not in (2, 3)]
                        first_in_bank = {0: True, 1: True, 2: True}
                        for jt in jts:
                            for hh in range(2):
                                extra = []
                                if need_cb[jt]:
                                    extra.append((anti, hsl(cbias[pb][jt], hh)))
                                dsts, sts = [], []
                                for hd3 in range(3):
                                    hd = hh * 3 + hd3
                                    bi = hd // 2
                                    dsts.append(cacc[bi][:, (hd % 2) * 193:(hd % 2) * 193 + 193])
                                    sts.append(first_in_bank[bi])
                                    first_in_bank[bi] = False
                                unit(kcmpT[:, g, jt * 128:(jt + 1) * 128], q3of(hh), extra, dsts,
                                     vc1[:, jt, g, :], sts, jt == jts[-1])

                        def cmp_evac(g=g, gi=gi, pb=pb, i=i, cacc=cacc, early=early, late=late):
                            branch_scales(cacc, 193, 2, gates_sb[pb][:, g * 6:g * 6 + 6], gi)

                            def imp_acc(hd, first):
                                src = cacc[hd // 2][:, (hd % 2) * 193 + 129:(hd % 2) * 193 + 193]
                                if first:
                                    P.ts("dve", imp[gi], src, rz[gi][:, hd:hd + 1], ALU.mult)
                                else:
                                    P.stt(imp[gi], src, rz[gi][:, hd:hd + 1], imp[gi], ALU.mult, ALU.add)

                            def y_out(hd):
                                src = cacc[hd // 2][:, (hd % 2) * 193:(hd % 2) * 193 + 128]
                                P.ts("dve", y_pre[pb][:, (g * 6 + hd) * 128:(g * 6 + hd + 1) * 128], src,
                                     aco[gi][:, hd:hd + 1], ALU.mult)

                            for n_, hd in enumerate(early):
                                imp_acc(hd, n_ == 0)
                            for hd in early:
                                y_out(hd)
                            for hd in late:
                                imp_acc(hd, False)
                            P.tt("dve", imp2[gi], imp[gi], keep_t[:, 64 - 2 * i:128 - 2 * i], ALU.mult)
                            P.tt("dve", imp2[gi], imp2[gi], addm_t[:, 64 - 2 * i:128 - 2 * i], ALU.add)
                            P.memset("dve", imp2[gi][:, 0:1], BIGV)
                            P.max8(m8[gi], imp2[gi])
                            P.ts("dve", negsel[gi], imp2[gi], m8[gi][:, 7:8], ALU.is_ge)
                            for hd in late:
                                y_out(hd)
                        pipe_push(cmp_evac)

                        def cmp_mask_T(gi=gi, i=i):
                            tb = rot()
                            tbb = tb.bf()
                            P.tr(tbb[0:64, 0:128], negsel[gi], ident)
                            P.copy("dve", selT[gi], tbb[0:64, 0:128])
                            for k0 in range(0, i + 1, 4):
                                nk = min(4, i + 1 - k0)
                                mb = rot()
                                for kk in range(nk):
                                    P.mm(mb[:, kk * 128:(kk + 1) * 128], E_sb[:, k0 + kk, :], selT[gi])
                                P.copy("act" if (k0 // 4) % 2 == 0 else "dve",
                                       maskT[gi][:, k0:k0 + nk, :].re("p k t -> p (k t)"), mb[:, 0:nk * 128])

                        kt0 = max(0, i - 4)
                        for kt in range(kt0, i + 1):
                            dl = 128 * (i - kt)
                            for hh in range(2):
                                extra = []
                                if dl in (0, 128, 512):
                                    extra.append((anti, hsl(Tb[dl], hh)))
                                dsts = [ACC[2 + hh][:, hd3 * 129:hd3 * 129 + 129] for hd3 in range(3)]
                                sts = [(kt == kt0 and hd3 == 0) for hd3 in range(3)]
                                unit(kw_slots[kt % 6][:, g, :], q3of(hh), extra, dsts, vw_slots[kt % 6][:, g, :],
                                     sts, kt == i)

                        def win_evac(g=g, gi=gi, pb=pb):
                            branch_scales(ACC[2:4], 129, 3, gates_sb[pb][:, 24 + g * 6:24 + g * 6 + 6], gi)
                            for hd in range(6):
                                src = ACC[2 + hd // 3][:, (hd % 3) * 129:(hd % 3) * 129 + 128]
                                yv = y_pre[pb][:, (g * 6 + hd) * 128:(g * 6 + hd + 1) * 128]
                                P.stt(yv, src, aco[gi][:, hd:hd + 1], yv, ALU.mult, ALU.add)
                        pipe_push(cmp_mask_T)
                        pipe_push(win_evac)

                    pipe_flush()
                    for g in range(2):
                        gi = g

                        def q3of(hh, g=g, qf=qf):
                            return qf[:, g * 6 + hh * 3:g * 6 + hh * 3 + 3, :].re("p h t -> p (h t)")

                        def hsl(tile_, hh, g=g):
                            return tile_[:, g * 6 + hh * 3:g * 6 + hh * 3 + 3, :].re("p h t -> p (h t)")

                        for hh in range(2):
                            for kt in range(i + 1):
                                dl = 128 * (i - kt)
                                extra = []
                                if dl <= 128:
                                    extra.append((anti, hsl(Tb[dl], hh)))
                                dsts = [ACC[hh][:, hd3 * 129:hd3 * 129 + 129] for hd3 in range(3)]
                                sts = [(kt == 0 and hd3 == 0) for hd3 in range(3)]
                                unit(ksT_all[:, g, kt * 128:(kt + 1) * 128], q3of(hh), extra, dsts,
                                     vs1_all[:, kt, g, :], sts, kt == i,
                                     mask=maskT[gi][:, kt:kt + 1, :].bcast([128, 3, 128]))

                            def sel_evac(g=g, gi=gi, pb=pb, hh=hh):
                                off = hh * 3
                                zsrc = ACC[hh][:, 128:128 + 129 * 2 + 1:129]
                                P.ts("dve", zt[gi][:, off:off + 3], zsrc, 1e-30, ALU.add)
                                P.recip(rz[gi][:, off:off + 3], zt[gi][:, off:off + 3])
                                P.tt("dve", aco[gi][:, off:off + 3], rz[gi][:, off:off + 3],
                                     gates_sb[pb][:, 12 + g * 6 + off:12 + g * 6 + off + 3], ALU.mult)
                                for hd3 in range(3):
                                    hd = off + hd3
                                    src = ACC[hh][:, hd3 * 129:hd3 * 129 + 128]
                                    yv = y_pre[pb][:, (g * 6 + hd) * 128:(g * 6 + hd + 1) * 128]
                                    P.stt(yv, src, aco[gi][:, hd:hd + 1], yv, ALU.mult, ALU.add)
                                if g == 0 and hh == 1:
                                    P.tt("pool", y_bf[pb][:, 0:768].wb(ybB[pb][0]), y_pre[pb][:, 0:768],
                                         sg_sb[pb][:, 0:768], ALU.mult)
                                if g == 1:
                                    c0 = 768 + hh * 384
                                    P.tt("dve", y_bf[pb][:, c0:c0 + 384].wb(ybB[pb][1 + hh]), y_pre[pb][:, c0:c0 + 384],
                                         sg_sb[pb][:, c0:c0 + 384], ALU.mult)
                            pipe_push(sel_evac)

                    def tile_epilogue(i=i, pb=pb, ht=ht):
                        out_proj(V(y_bf[pb].ap, ybB[pb]), w_out, ht, ht, pb)
                        rms_stats(ht, rstd[pb], junk, ss[pb], rtt[pb])
                        P.stt(out_t[pb], ht, rstd[pb], g_fin, ALU.mult, ALU.mult)
                        P.dma("sp", out_d[i * 128:(i + 1) * 128, :], out_t[pb], waw=False)
                    pipe_push(tile_epilogue)
                pipe_flush()

    P.emit()
    return nc


_PROG = {}


def _get_prog(stage):
    if stage not in _PROG:
        _PROG[stage] = build_program(stage)
    return _PROG[stage]


def _in_maps(inputs):
    C = make_consts()
    shared = {}
    for k, shp in IN_SHAPES.items():
        if k in ("x", "mem"):
            continue
        a = np.asarray(inputs[k], dtype=np.float32)
        if k in ("norm_g",):
            a = a.reshape(shp)
        elif a.ndim >= 1 and a.shape[0] == 1 and list(a.shape[1:]) == shp[-(a.ndim - 1):] and a.ndim - 1 == len(shp):
            a = a[0]
        a = np.ascontiguousarray(a.reshape(shp))
        shared[k] = a
    for k, shp in CONST_SHAPES.items():
        shared[k] = np.ascontiguousarray(C[k].reshape(shp).astype(np.float32))
    maps = []
    x = np.asarray(inputs["x"], dtype=np.float32)
    mem = np.asarray(inputs["mem"], dtype=np.float32)
    for b in range(8):
        m = dict(shared)
        m["x"] = np.ascontiguousarray(x[b])
        m["mem"] = np.ascontiguousarray(mem[b])
        maps.append(m)
    return maps


def run_stage(inputs, stage, cores=8):
    nc = _get_prog(stage)
    maps = _in_maps(inputs)[:cores]
    res = run_bass_kernel_spmd(nc, maps, core_ids=list(range(cores)))
    return res.results


def kernel(**inputs):
    res = run_stage(inputs, "full")
    return np.stack([r["out"] for r in res], axis=0).astype(np.float32)
```

```python
import math
from contextlib import ExitStack, contextmanager

import numpy as np
import concourse.bass as bass
import concourse.mybir as mybir
from concourse.bass_utils import run_bass_kernel_spmd

F32 = mybir.dt.float32
BF16 = mybir.dt.bfloat16
AF = mybir.ActivationFunctionType
ALU = mybir.AluOpType
AX = mybir.AxisListType

S = 4096
D = 1024
NT = S // 128
NEG = -30000.0
BIGV = 1.0e30
FOFF = 2176
FLEN = 4608
EPS = 1e-6


class Buf:
    __slots__ = ("name", "ws", "rs", "excl")

    def __init__(self, name, excl=False):
        self.name = name
        self.ws = []
        self.rs = []
        self.excl = excl


class V:
    __slots__ = ("ap", "bufs")

    def __init__(self, ap, bufs):
        self.ap = ap
        self.bufs = tuple(bufs)

    def __getitem__(self, idx):
        return V(self.ap[idx], self.bufs)

    def re(self, s, **kw):
        return V(self.ap.rearrange(s, **kw), self.bufs)

    def bcast(self, shape):
        return V(self.ap.to_broadcast(list(shape)), self.bufs)

    def wb(self, *bufs):
        return V(self.ap, bufs)

    def bf(self):
        return V(self.ap.bitcast(BF16), self.bufs)

    @property
    def shape(self):
        return self.ap.shape


class Inst:
    __slots__ = ("eng", "fn", "deps", "dma", "signal", "tick", "dma_n", "idx")


COMPUTE = ("pe", "act", "dve", "pool")
DMAQ = ("sp", "pool", "act")
ALLENG = ("sp", "pe", "act", "dve", "pool")
RING = 24


class Prog:
    def __init__(self, nc):
        self.nc = nc
        self.insts = []
        self.eng_obj = {"pe": nc.tensor, "act": nc.scalar, "dve": nc.vector,
                        "pool": nc.gpsimd, "sp": nc.sync}
        self.nbuf = 0
        self.stacks = []
        self.last = {e: None for e in ALLENG}
        self.dma_since = []

    def buf(self, name=None, excl=False):
        self.nbuf += 1
        return Buf(name or f"b{self.nbuf}", excl)

    @contextmanager
    def scope(self):
        es = ExitStack()
        self.stacks.append(es)
        try:
            yield
        finally:
            self.barrier()
            self.stacks.pop()
            es.close()

    def sbuf(self, name, shape, dtype):
        t = self.stacks[-1].enter_context(self.nc.sbuf_tensor(name, list(shape), dtype))
        return V(t[tuple(slice(None) for _ in shape)], [self.buf(name)])

    def psum(self, name, shape, dtype):
        t = self.nc.alloc_psum_tensor(name, list(shape), dtype)
        return V(t[tuple(slice(None) for _ in shape)], [self.buf(name, excl=True)])

    def dram(self, name, shape, dtype, kind="Internal"):
        t = self.nc.dram_tensor(name, list(shape), dtype, kind=kind)
        return V(t.ap(), [self.buf(name)])

    def add(self, eng, fn, reads, writes, dma=False, waw=True):
        idx = len(self.insts)
        ins = Inst()
        ins.eng, ins.fn, ins.dma, ins.signal, ins.idx = eng, fn, dma, False, idx
        ins.tick = None
        ins.dma_n = None
        deps = set()
        rb, wbufs = [], []
        for v in reads:
            for b in v.bufs:
                (wbufs if b.excl else rb).append(b)
        for v in writes:
            wbufs.extend(v.bufs)
        for b in rb:
            for w in b.ws:
                deps.add((w, "raw"))
        for b in wbufs:
            if waw or b.excl:
                for w in b.ws:
                    deps.add((w, "raw" if b.excl else "waw"))
            for r in b.rs:
                deps.add((r, "war"))
        for b in rb:
            b.rs.append(idx)
        for b in wbufs:
            if waw or b.excl:
                b.ws = [idx]
                b.rs = []
            else:
                b.ws.append(idx)
        keep = {}
        for d, kind in deps:
            if d == idx:
                continue
            di = self.insts[d]
            if not di.dma and not dma and di.eng == eng:
                if eng == "pe":
                    continue
                if kind != "raw":
                    continue
            if di.dma:
                keep[("dma", d)] = d
            else:
                k = ("eng", di.eng)
                if k not in keep or keep[k] < d:
                    keep[k] = d
        ins.deps = sorted(keep.values())
        for d in ins.deps:
            self.insts[d].signal = True
        self.insts.append(ins)
        if dma:
            self.dma_since.append(idx)
        else:
            self.last[eng] = idx
        return ins

    def barrier(self):
        lasts = [v for v in self.last.values() if v is not None]
        dmas = list(self.dma_since)
        self.dma_since = []
        for e in ALLENG:
            ins = Inst()
            ins.eng, ins.fn, ins.dma, ins.signal, ins.idx = e, None, False, False, len(self.insts)
            ins.tick = None
            ins.dma_n = None
            ins.deps = sorted(set([d for d in lasts if self.insts[d].eng != e] + dmas))
            for d in ins.deps:
                self.insts[d].signal = True
            self.insts.append(ins)

    def mm(self, out, lhsT, rhs, start=True, stop=True, **kw):
        nc = self.nc
        return self.add("pe", lambda: nc.tensor.matmul(out.ap, lhsT.ap, rhs.ap, start=start, stop=stop, **kw),
                        [lhsT, rhs], [out])

    def tr(self, out, in_, ident):
        nc = self.nc
        return self.add("pe", lambda: nc.tensor.transpose(out.ap, in_.ap, ident.ap), [in_, ident], [out])

    def act(self, out, in_, func, bias=None, scale=None, accum_out=None):
        nc = self.nc
        kw = {}
        reads = [in_]
        writes = [out]
        if bias is not None:
            if isinstance(bias, V):
                kw["bias"] = bias.ap
                reads.append(bias)
            else:
                kw["bias"] = bias
        if scale is not None:
            if isinstance(scale, V):
                kw["scale"] = scale.ap
                reads.append(scale)
            else:
                kw["scale"] = scale
        if accum_out is not None:
            kw["accum_out"] = accum_out.ap
            writes.append(accum_out)
        return self.add("act", lambda: nc.scalar.activation(out.ap, in_.ap, func, **kw), reads, writes)

    def copy(self, eng, out, in_):
        e = self.eng_obj[eng]
        if eng == "act":
            return self.add("act", lambda: e.copy(out.ap, in_.ap), [in_], [out])
        return self.add(eng, lambda: e.tensor_copy(out.ap, in_.ap), [in_], [out])

    def tt(self, eng, out, in0, in1, op):
        e = self.eng_obj[eng]
        return self.add(eng, lambda: e.tensor_tensor(out.ap, in0.ap, in1.ap, op), [in0, in1], [out])

    def ts(self, eng, out, in0, s1, op0, s2=None, op1=None):
        e = self.eng_obj[eng]
        reads = [in0]
        a1, a2 = s1, s2
        if isinstance(s1, V):
            reads.append(s1)
            a1 = s1.ap
        if isinstance(s2, V):
            reads.append(s2)
            a2 = s2.ap
        kw = {}
        if op1 is not None:
            kw["op1"] = op1
        return self.add(eng, lambda: e.tensor_scalar(out.ap, in0.ap, a1, a2, op0, **kw), reads, [out])

    def stt(self, out, in0, scalar, in1, op0, op1):
        e = self.nc.vector
        reads = [in0, in1]
        sc = scalar
        if isinstance(scalar, V):
            reads.append(scalar)
            sc = scalar.ap
        return self.add("dve", lambda: e.scalar_tensor_tensor(out.ap, in0.ap, sc, in1.ap, op0, op1), reads, [out])

    def memset(self, eng, out, val):
        e = self.eng_obj[eng]
        return self.add(eng, lambda: e.memset(out.ap, val), [], [out])

    def max8(self, out, in_):
        nc = self.nc
        return self.add("dve", lambda: nc.vector.max(out.ap, in_.ap), [in_], [out])

    def recip(self, out, in_):
        nc = self.nc
        return self.add("dve", lambda: nc.vector.reciprocal(out.ap, in_.ap), [in_], [out])

    def bn_stats(self, out, in_):
        nc = self.nc
        return self.add("dve", lambda: nc.vector.bn_stats(out.ap, in_.ap), [in_], [out])

    def bn_aggr(self, out, in_):
        nc = self.nc
        return self.add("dve", lambda: nc.vector.bn_aggr(out.ap, in_.ap), [in_], [out])

    def dma(self, q, out, in_, waw=True, **kw):
        e = self.eng_obj[q]
        return self.add(q, lambda: e.dma_start(out.ap, in_.ap, **kw), [in_], [out], dma=True, waw=waw)

    def emit(self):
        nc = self.nc
        insts = self.insts
        tick = {e: 0 for e in COMPUTE}
        dman = {q: 0 for q in DMAQ}
        for ins in insts:
            if ins.dma:
                ins.dma_n = dman[ins.eng]
                dman[ins.eng] += 1
            elif ins.signal:
                tick[ins.eng] += 1
                ins.tick = tick[ins.eng]
        sems = {e: nc.alloc_semaphore(f"sem_{e}") for e in COMPUTE}
        rings = {}
        for q in DMAQ:
            if dman[q]:
                rings[q] = [nc.alloc_semaphore(f"ring_{q}_{i}") for i in range(min(RING, dman[q]))]
        per_eng = {e: [] for e in ALLENG}
        for ins in insts:
            per_eng[ins.eng].append(ins)

        def run_engine(ename):
            eobj = self.eng_obj[ename]
            known = {}

            def wait(sem, key, val):
                if known.get(key, 0) >= val:
                    return
                known[key] = val
                eobj.wait_ge(sem, val)

            for ins in per_eng[ename]:
                for d in ins.deps:
                    di = insts[d]
                    if di.dma:
                        r = rings[di.eng]
                        k = di.dma_n % len(r)
                        wait(r[k], ("ring", di.eng, k), 16 * (di.dma_n // len(r) + 1))
                    elif di.tick is not None:
                        wait(sems[di.eng], ("eng", di.eng), di.tick)
                if ins.fn is None:
                    continue
                if ins.dma:
                    r = rings[ins.eng]
                    k = ins.dma_n % len(r)
                    gen = ins.dma_n // len(r)
                    if gen > 0:
                        wait(r[k], ("ring", ins.eng, k), 16 * gen)
                    ins.fn().then_inc(r[k], 16)
                else:
                    o = ins.fn()
                    if ins.signal:
                        o.then_inc(sems[ename], 1)
            if ename in rings:
                r = rings[ename]
                n = dman[ename]
                for k in range(len(r)):
                    cnt = (n - k + len(r) - 1) // len(r)
                    if cnt > 0:
                        wait(r[k], ("ring", ename, k), 16 * cnt)

        with nc.Block() as block:
            @block.sync
            def _(e):
                run_engine("sp")

            @block.tensor
            def _(e):
                run_engine("pe")

            @block.scalar
            def _(e):
                run_engine("act")

            @block.vector
            def _(e):
                run_engine("dve")

            @block.gpsimd
            def _(e):
                run_engine("pool")
        return nc


def _bucket(n):
    n = max(int(n), 0)
    if n < 16:
        return n
    nf = np.float32(max(n, 1))
    v = np.log(nf / np.float32(16.0)) / np.float32(math.log(128 / 16)) * np.float32(16.0)
    return min(16 + int(np.float32(v)), 31)


_CONSTS = None


def make_consts():
    global _CONSTS
    if _CONSTS is not None:
        return _CONSTS
    f32 = np.float32
    c = {}
    c["c_ident"] = np.eye(128, dtype=f32)
    c["c_anti"] = np.ascontiguousarray(np.eye(128, dtype=f32)[::-1])
    half = 64
    inv = (np.float32(10000.0) ** (-np.arange(half, dtype=f32) / np.float32(half))).astype(f32)
    ang = (np.arange(S, dtype=f32)[:, None] * inv[None, :]).astype(f32)
    cs = np.cos(ang).astype(f32).reshape(NT, 128, half).transpose(1, 0, 2)
    sn = np.sin(ang).astype(f32).reshape(NT, 128, half).transpose(1, 0, 2)
    c["c_cos"] = np.ascontiguousarray(cs)
    c["c_sin"] = np.ascontiguousarray(sn)
    gam = 1.0 - 2.0 ** (-5.0 - np.arange(4))
    scale = 128.0 ** -0.5
    m = np.arange(128)[:, None, None]
    cc = np.arange(128)[None, None, :]
    gh = gam[None, :, None]
    decT = np.where(cc >= m, scale * gh ** np.maximum(cc - m, 0), 0.0)
    c["c_decT"] = decT.astype(f32)
    xi = scale * gh ** (cc + 1.0)
    c["c_xi"] = np.ascontiguousarray(np.broadcast_to(xi, (128, 4, 128))).astype(f32)
    c["c_zeta"] = (gam[None, :] ** (127.0 - np.arange(128)[:, None])).astype(f32)
    c["gchunk"] = [float(g ** 128) for g in gam]
    A = np.zeros((128, 12, 128), f32)
    tp = np.arange(128)[:, None]
    t = np.arange(128)[None, :]
    for wi, w in enumerate((2, 4, 8, 16)):
        cur = np.where((tp > t - w) & (tp <= t), 1.0 / w, 0.0) - (tp == t)
        prev = np.where(tp > 128 + t - w, 1.0 / w, 0.0)
        cnt = np.minimum(t + 1, w)
        first = np.where((tp >= np.maximum(0, t - w + 1)) & (tp <= t), 1.0 / cnt, 0.0) - (tp == t)
        A[:, wi * 3 + 0, :] = cur
        A[:, wi * 3 + 1, :] = prev
        A[:, wi * 3 + 2, :] = first
    c["c_poolA"] = A
    n_cmp, n_slc = 255, 64
    cst = np.arange(n_cmp)[:, None] * 16
    sst = np.arange(n_slc)[None, :] * 64
    ov = np.clip(np.minimum(cst + 32, sst + 64) - np.maximum(cst, sst), 0, None) / 16.0
    o1 = np.zeros((256, 65), f32)
    o1[:255, 0] = 1.0
    o1[:255, 1:] = ov
    c["c_ovl1"] = np.ascontiguousarray(o1.reshape(2, 128, 65).transpose(1, 0, 2))
    E = np.zeros((64, 32, 128), f32)
    for kt in range(32):
        E[2 * kt, kt, :64] = 1.0
        E[2 * kt + 1, kt, 64:] = 1.0
    c["c_E"] = E
    oh = np.zeros((2, 33, FLEN), f32)
    for idx in range(FLEN):
        n = idx - FOFF
        for var in range(2):
            masked = (n < 0) or (var == 1 and n >= 512)
            if masked:
                oh[var, 32, idx] = 1.0
            else:
                oh[var, _bucket(n), idx] += 1.0
                oh[var, 31, idx] -= 1.0
    c["c_oh"] = oh
    q = np.arange(128)[:, None]
    rel = np.arange(128)[None, :] - 64
    cur = (q >= 64).astype(np.int64)
    forced = (rel == cur) | (rel == cur - 1)
    fut = rel > cur
    c["c_keep"] = np.where(forced | fut, 0.0, 1.0).astype(f32)
    c["c_addm"] = np.where(forced, BIGV, np.where(fut, -BIGV, 0.0)).astype(f32)
    _CONSTS = c
    return c


CONST_SHAPES = {
    "c_ident": [128, 128], "c_anti": [128, 128], "c_cos": [128, NT, 64], "c_sin": [128, NT, 64],
    "c_decT": [128, 4, 128], "c_xi": [128, 4, 128], "c_zeta": [128, 4], "c_poolA": [128, 12, 128],
    "c_ovl1": [128, 2, 65], "c_E": [64, 32, 128], "c_oh": [2, 33, FLEN], "c_keep": [128, 128],
    "c_addm": [128, 128],
}

IN_SHAPES = {
    "x": [S, D], "mem": [256, D], "norm_g": [2, D], "final_g": [1, D], "mem_norm_g": [1, D],
    "rel_bias": [32, 12], "ev_w_in": [D, 5120], "ev_pool_w": [4, 192, 192], "ev_pool_scale": [1, 768],
    "ev_w_mem_kv": [D, 1024], "ev_w_out": [2048, D], "od_w_in": [D, 5668], "od_cmp_pe": [2, 32, 128],
    "od_cmp_w1": [2, 4096, 256], "od_cmp_b1": [2, 256], "od_cmp_w2": [2, 256, 128],
    "od_w_mem_kv": [D, 1024], "od_w_out": [2048, D],
}


def build_program(stage="full"):
    nc = bass.Bass("TRN2", target_bir_lowering=False)
    P = Prog(nc)
    C = make_consts()
    I = {k: P.dram(k, shp, F32, kind="ExternalInput") for k, shp in IN_SHAPES.items()}
    K = {k: P.dram(k, shp, F32, kind="ExternalInput") for k, shp in CONST_SHAPES.items()}
    out_d = P.dram("out", [S, D], F32, kind="ExternalOutput")
    h1_d = P.dram("h1", [S, D], F32, kind="ExternalOutput" if stage == "even" else "Internal")
    h1_tiles = [h1_d[i * 128:(i + 1) * 128, :].wb(P.buf(f"h1_{i}")) for i in range(NT)]

    banks = [P.psum(f"bank{i}", [128, 512], F32) for i in range(8)]
    rot_state = {"n": 0, "set": list(range(8))}

    def rot():
        s = rot_state["set"]
        b = banks[s[rot_state["n"] % len(s)]]
        rot_state["n"] += 1
        return b

    def bcast_rows(v, row, n):
        ap = bass.AP(v.ap.tensor, row * n, [[0, 128], [1, n]])
        return V(ap, v.bufs)

    with P.scope():
        ident = P.sbuf("ident", [128, 128], BF16)
        P.dma("pool", ident, K["c_ident"])
        eps_t = P.sbuf("eps_t", [128, 1], F32)
        P.memset("dve", eps_t, EPS)
        mkT = [P.sbuf(f"mkT{l}", [128, 4, 256], BF16) for l in range(2)]
        mv1 = [P.sbuf(f"mv1{l}", [128, 2, 4, 129], BF16) for l in range(2)]

        mhalf = P.sbuf("mhalf", [128, 4], F32)
        P.memset("pool", mhalf, -0.5)

        def rms_stats(ht, rstd, junk, ss, rt):
            P.act(junk, ht, AF.Square, accum_out=ss)
            P.ts("dve", rt, ss, 1.0 / D, ALU.mult, EPS, ALU.add)
            P.tt("pool", rstd, rt, mhalf[:, 0:1], ALU.pow)

        with P.scope():
            mg = P.sbuf("mg", [128, D], F32)
            P.dma("sp", mg, bcast_rows(I["mem_norm_g"], 0, D))
            junk = P.sbuf("junk0", [128, D], F32)
            memT = P.sbuf("memT", [128, 8, 256], BF16)
            for mc in range(2):
                mt = P.sbuf(f"mt{mc}", [128, D], F32)
                P.dma("sp", mt, I["mem"][mc * 128:(mc + 1) * 128, :])
                ss = P.sbuf(f"mss{mc}", [128, 1], F32)
                rt = P.sbuf(f"mrt{mc}", [128, 1], F32)
                rstd = P.sbuf(f"mrs{mc}", [128, 1], F32)
                rms_stats(mt, rstd, junk, ss, rt)
                mn = P.sbuf(f"mn{mc}", [128, D], BF16)
                P.stt(mn, mt, rstd, mg, ALU.mult, ALU.mult)
                bk = rot()
                bb = bk.bf()
                for k in range(8):
                    P.tr(bb[:, k * 128:(k + 1) * 128], mn[:, k * 128:(k + 1) * 128], ident)
                P.copy("dve", memT[:, :, mc * 128:(mc + 1) * 128],
                       bb.re("p (k t) -> p k t", k=8))
            for l, wname in enumerate(("ev_w_mem_kv", "od_w_mem_kv")):
                wkv = P.sbuf(f"wkv{l}", [128, 8, 1024], BF16)
                for k in range(8):
                    P.dma("pool", wkv[:, k, :], I[wname][k * 128:(k + 1) * 128, :], waw=False,
                          max_dma_last_dim=4096)
                for h in range(4):
                    bk = rot()
                    for k in range(8):
                        P.mm(bk[:, 0:256], wkv[:, k, h * 128:(h + 1) * 128], memT[:, k, :],
                             start=(k == 0), stop=(k == 7))
                    P.copy("act", mkT[l][:, h, :], bk[:, 0:256])
                P.memset("pool", mv1[l], 1.0)
                for mc in range(2):
                    bk = rot()
                    for k in range(8):
                        P.mm(bk[:, 0:512], memT[:, k, mc * 128:(mc + 1) * 128], wkv[:, k, 512:1024],
                             start=(k == 0), stop=(k == 7))
                    P.copy("dve", mv1[l][:, mc, :, 0:128], bk[:, 0:512].re("p (h d) -> p h d", h=4))

        def mem_attention(l, xqT_sb, y_dst, tag, pb, gate=None, between=None):
            pT = mem_pT[pb]
            bks = [rot(), rot()]
            for h in range(4):
                for mc in range(2):
                    ci = h * 2 + mc
                    P.mm(bks[ci // 4][:, (ci % 4) * 128:(ci % 4 + 1) * 128],
                         mkT[l][:, h, mc * 128:(mc + 1) * 128], xqT_sb[:, h, :])
            for j in range(2):
                P.act(pT[:, j * 4:(j + 1) * 4, :].re("p a t -> p (a t)"), bks[j], AF.Exp)
            if between is not None:
                between()
            oms = [rot(), rot()]
            for h in range(4):
                ob = oms[h // 2][:, (h % 2) * 129:(h % 2) * 129 + 129]
                for mc in range(2):
                    P.mm(ob, pT[:, h * 2 + mc, :], mv1[l][:, mc, h, :],
                         start=(mc == 0 and h % 2 == 0), stop=(mc == 1), skip_group_check=True)
            rs = mem_rs[pb]
            for h in range(4):
                ob = oms[h // 2]
                P.recip(rs[:, h:h + 1], ob[:, (h % 2) * 129 + 128:(h % 2) * 129 + 129])
            for h in range(4):
                ob = oms[h // 2]
                if gate is None:
                    P.ts("dve", y_dst[:, h * 128:(h + 1) * 128], ob[:, (h % 2) * 129:(h % 2) * 129 + 128],
                         rs[:, h:h + 1], ALU.mult)
                else:
                    P.stt(y_dst[:, h * 128:(h + 1) * 128], ob[:, (h % 2) * 129:(h % 2) * 129 + 128],
                          rs[:, h:h + 1], gate[:, h * 128:(h + 1) * 128], ALU.mult, ALU.mult)

        def out_proj(y_bf, w_out, h_in, h_out, pb, scale=0.5):
            yT = yT_t[pb]
            for half in range(2):
                bk = rot()
                bb = bk.bf()
                for k in range(8):
                    kk = half * 8 + k
                    P.tr(bb[:, k * 128:(k + 1) * 128], y_bf[:, kk * 128:(kk + 1) * 128], ident)
                P.copy("act" if half == 0 else "dve", yT[:, half * 8:(half + 1) * 8, :].re("p k t -> p (k t)"), bb)
            for n in range(2):
                bk = rot()
                for k in range(16):
                    P.mm(bk, yT[:, k, :], w_out[:, k, n * 512:(n + 1) * 512], start=(k == 0), stop=(k == 15))
                P.stt(h_out[:, n * 512:(n + 1) * 512], bk, scale, h_in[:, n * 512:(n + 1) * 512], ALU.mult, ALU.add)

        with P.scope():
            w_in = P.sbuf("ev_w_in_sb", [128, 8, 5120], BF16)
            for k in range(8):
                P.dma("pool", w_in[:, k, :], I["ev_w_in"][k * 128:(k + 1) * 128, :],
                      waw=False, max_dma_last_dim=4096)
            w_out = P.sbuf("ev_w_out_sb", [128, 16, 1024], BF16)
            for k0 in range(0, 16, 4):
                P.dma("pool", w_out[:, k0:k0 + 4, :],
                      I["ev_w_out"][k0 * 128:(k0 + 4) * 128, :].re("(k p) c -> p k c", p=128), waw=False,
                      max_dma_last_dim=4096)
            g_ev = P.sbuf("g_ev", [128, D], F32)
            P.dma("sp", g_ev, bcast_rows(I["norm_g"], 0, D))
            poolw = P.sbuf("poolw", [96, 4, 2, 192], BF16)
            with P.scope():
                pscale = P.sbuf("pscale", [128, 768], F32)
                P.dma("sp", pscale, bcast_rows(I["ev_pool_scale"], 0, 768))
                poolw0 = P.sbuf("poolw0", [96, 4, 2, 192], F32)
                for g in range(4):
                    P.dma("sp", poolw0[:, g, :, :], I["ev_pool_w"][g].re("(cc p) d -> p cc d", p=96), waw=False)
                for g in range(4):
                    for cc in range(2):
                        P.tt("dve", poolw[:, g, cc, :], poolw0[:, g, cc, :], pscale[0:96, g * 192:(g + 1) * 192], ALU.mult)
            poolA = P.sbuf("poolA", [128, 12, 128], BF16)
            P.dma("pool", poolA, K["c_poolA"])
            cos_t = [P.sbuf(f"cos_t{j}", [128, 1, 64], F32) for j in range(2)]
            sin_t = [P.sbuf(f"sin_t{j}", [128, 1, 64], F32) for j in range(2)]
            decT = P.sbuf("decT", [128, 4, 128], BF16)
            xi_t = P.sbuf("xi_t", [128, 4, 128], BF16)
            zeta = P.sbuf("zeta", [128, 4], F32)
            P.dma("pool", decT, K["c_decT"])
            P.dma("pool", xi_t, K["c_xi"])
            P.dma("sp", zeta, K["c_zeta"])
            Rst = P.sbuf("Rst", [128, 4, 192], F32)
            Rbf = P.sbuf("Rbf", [128, 4, 192], BF16)
            P.memset("dve", Rst, 0.0)
            P.memset("dve", Rbf, 0.0)
            gch = C["gchunk"]

            print("sbuf remaining before even work tiles", nc.sbuf_bytes_remaining)
            hT = [P.sbuf(f"hT{j}", [128, D], F32) for j in range(3)]
            u_bf = P.sbuf("u_bf", [128, D], BF16)
            junk = u_bf
            th_t = P.sbuf("th_t", [128, 512], F32)
            ss = [P.sbuf(f"ss{j}", [128, 1], F32) for j in range(2)]
            rtt = [P.sbuf(f"rtt{j}", [128, 1], F32) for j in range(2)]
            rstd = [P.sbuf(f"rstd{j}", [128, 1], F32) for j in range(2)]
            uTt = P.sbuf("uT", [128, 8, 128], BF16)
            za = [P.sbuf(f"za{j}", [128, 768], BF16) for j in range(3)]
            qk_f = P.sbuf("qk_f", [128, 1024], F32)
            rt1 = P.sbuf("rt1", [128, 8, 64], BF16)
            rt2 = P.sbuf("rt2", [128, 8, 64], BF16)
            rt3 = P.sbuf("rt3", [128, 8, 64], BF16)
            rt4 = P.sbuf("rt4", [128, 8, 64], BF16)
            qk_rot = P.sbuf("qk_rot", [128, 8, 2, 64], BF16)
            kz = [P.sbuf(f"kz{j}", [128, 4, 128], BF16) for j in range(2)]
            qT = [P.sbuf(f"qT{j}", [128, 4, 128], BF16) for j in range(2)]
            qxT = [P.sbuf(f"qxT{j}", [128, 4, 128], BF16) for j in range(2)]
            kT = [P.sbuf(f"kT{j}", [128, 4, 128], BF16) for j in range(2)]
            v_bf = [P.sbuf(f"v_bf{j}", [128, 768], BF16) for j in range(2)]
            sg = [P.sbuf(f"sg{j}", [128, 2048], BF16) for j in range(2)]
            xqT_sb = [P.sbuf(f"xqT_sb{j}", [128, 4, 128], BF16) for j in range(2)]
            att = P.sbuf("att", [128, 4, 128], BF16)
            bst = P.sbuf("bst", [128, 4, 6], F32)
            mv = P.sbuf("mv", [128, 4, 2], F32)
            grt = P.sbuf("grt", [128, 4], F32)
            grs = P.sbuf("grs", [128, 4], F32)
            tmpR = P.sbuf("tmpR", [128, 4, 192], F32)
            y_bf = P.sbuf("y_bf", [128, 2048], BF16)
            ybufs = [P.buf(f"ybuf{j}") for j in range(7)]
            tbufs = [P.buf(f"tbuf{j}") for j in range(4)]
            y_bf_all = V(y_bf.ap, ybufs)
            pooledT = P.sbuf("pooledT", [96, 8, 128], BF16)
            mem_pT = [P.sbuf("mem_pT", [128, 8, 128], BF16)] * 2
            mem_rs = [P.sbuf("mem_rs", [128, 4], F32)] * 2
            yT_t = [P.sbuf("yT", [128, 16, 128], BF16)] * 2
            print("sbuf remaining after even work tiles", nc.sbuf_bytes_remaining)

            def ev_load(i):
                P.dma("sp", hT[i % 3], I["x"][i * 128:(i + 1) * 128, :])
                P.dma("sp", cos_t[i % 2], K["c_cos"][:, i:i + 1, :])
                P.dma("sp", sin_t[i % 2], K["c_sin"][:, i:i + 1, :])

            ev_load(0)

            def ev_head(i):
                rms_stats(hT[i % 3], rstd[i % 2], junk, ss[i % 2], rtt[i % 2])
                P.stt(u_bf, hT[i % 3], rstd[i % 2], g_ev, ALU.mult, ALU.mult)

            ev_head(0)

            def ev_stage_a(i):
                pb = i % 2
                ht = hT[i % 3]
                if i + 1 < NT:
                    ev_load(i + 1)
                bk = rot()
                bb = bk.bf()
                for k in range(8):
                    P.tr(bb[:, k * 128:(k + 1) * 128], u_bf[:, k * 128:(k + 1) * 128], ident)
                P.copy("act", uTt.re("p k t -> p (k t)"), bb)

                def proj_tok(c0, width):
                    b = rot()
                    for k in range(8):
                        P.mm(b[:, 0:width], uTt[:, k, :], w_in[:, k, c0:c0 + width], start=(k == 0), stop=(k == 7))
                    return b

                b = proj_tok(768, 512)
                P.copy("act", qk_f[:, 0:512], b)
                b = proj_tok(1280, 512)
                P.copy("dve", qk_f[:, 512:1024], b)
                qv = qk_f.re("p (h two j) -> p h two j", h=8, two=2)
                x1 = qv[:, :, 0, :]
                x2 = qv[:, :, 1, :]
                cb = cos_t[pb].bcast([128, 8, 64])
                sb_ = sin_t[pb].bcast([128, 8, 64])
                P.tt("dve", rt1, x1, cb, ALU.mult)
                P.tt("pool", rt2, x2, sb_, ALU.mult)
                P.tt("pool", rt4, x2, cb, ALU.mult)
                P.tt("dve", rt3, x1, sb_, ALU.mult)
                P.tt("dve", qk_rot[:, :, 0, :], rt1, rt2, ALU.subtract)
                P.tt("dve", qk_rot[:, :, 1, :], rt3, rt4, ALU.add)
                qkr = qk_rot.re("p h two j -> p (h two j)")
                P.tt("pool", kz[pb], qkr[:, 512:1024].re("p (h d) -> p h d", h=4),
                     zeta[:, :].re("p (h o) -> p h o", o=1).bcast([128, 4, 128]), ALU.mult)
                b = proj_tok(0, 512)
                P.copy("act", za[i % 3][:, 0:512], b)
                b = proj_tok(512, 256)
                P.copy("dve", za[i % 3][:, 512:768], b[:, 0:256])
                b = proj_tok(1792, 512)
                P.copy("act", v_bf[pb][:, 0:512], b)
                b = proj_tok(2304, 256)
                P.copy("dve", v_bf[pb][:, 512:768], b[:, 0:256])
                if i + 1 < NT:
                    ev_head(i + 1)
                for j in range(4):
                    b = proj_tok(3072 + 512 * j, 512)
                    P.act(th_t, b, AF.Tanh, scale=0.5)
                    P.stt(sg[pb][:, j * 512:(j + 1) * 512], th_t, 1.0, b, ALU.add, ALU.mult)
                b = rot()
                for h in range(4):
                    for k in range(8):
                        P.mm(b[:, h * 128:(h + 1) * 128], w_in[:, k, 2560 + h * 128:2560 + (h + 1) * 128], uTt[:, k, :],
                             start=(k == 0), stop=(k == 7))
                P.ts("dve", xqT_sb[pb].re("p h t -> p (h t)"), b, 128.0 ** -0.5, ALU.mult)
                bk = rot()
                bb = bk.bf()
                for j in range(8):
                    P.tr(bb[:, j * 128:(j + 1) * 128], qkr[:, j * 128:(j + 1) * 128], ident)
                P.copy("act", qT[pb].re("p h t -> p (h t)"), bb[:, 0:512])
                P.tt("dve", qxT[pb].re("p h t -> p (h t)"), bb[:, 0:512], xi_t.re("p h t -> p (h t)"), ALU.mult)
                P.copy("act", kT[pb].re("p h t -> p (h t)"), bb[:, 512:1024])

            def ev_stage_b(i):
                pb = i % 2
                ht = hT[i % 3]
                zc = za[i % 3]
                zp = za[(i - 1) % 3]
                bk = rot()
                for h in range(4):
                    P.mm(bk[:, h * 128:(h + 1) * 128], kT[pb][:, h, :], qT[pb][:, h, :])
                P.tt("dve", att.re("p h t -> p (h t)"), bk, decT.re("p h t -> p (h t)"), ALU.mult)
                ppb = [rot(), rot()]
                for g in range(4):
                    for cc in range(2):
                        ci = g * 2 + cc
                        dst = ppb[ci // 4][0:96, (ci % 4) * 128:(ci % 4 + 1) * 128]
                        c0 = g * 192 + cc * 96
                        if i == 0:
                            P.mm(dst, zc[:, c0:c0 + 96], poolA[:, g * 3 + 2, :])
                        else:
                            P.mm(dst, zp[:, c0:c0 + 96], poolA[:, g * 3 + 1, :], start=True, stop=False)
                            P.mm(dst, zc[:, c0:c0 + 96], poolA[:, g * 3 + 0, :], start=False, stop=True)
                for j in range(2):
                    P.copy("act", pooledT[:, j * 4:(j + 1) * 4, :].re("p a t -> p (a t)"), ppb[j][0:96, :])
                obk = [banks[0], banks[1]]
                kvb = [banks[2], banks[3]]

                def between():
                    for h in range(4):
                        ob = obk[h // 2][:, (h % 2) * 192:(h % 2) * 192 + 192]
                        P.mm(ob, att[:, h, :], v_bf[pb][:, h * 192:(h + 1) * 192], start=True, stop=False)
                        P.mm(ob, qxT[pb][:, h, :], Rbf[:, h, :], start=False, stop=True)
                    for h in range(4):
                        kb = kvb[h // 2][:, (h % 2) * 192:(h % 2) * 192 + 192]
                        P.mm(kb, kz[pb][:, h, :], v_bf[pb][:, h * 192:(h + 1) * 192])
                    ypb = [rot(), rot()]
                    for g in range(4):
                        yb = ypb[g // 2][:, (g % 2) * 192:(g % 2) * 192 + 192]
                        for cc in range(2):
                            P.mm(yb, pooledT[:, g * 2 + cc, :], poolw[:, g, cc, :], start=(cc == 0), stop=(cc == 1))
                    for h in range(4):
                        ob = obk[h // 2][:, (h % 2) * 192:(h % 2) * 192 + 192]
                        P.bn_stats(bst[:, h, :], ob)
                    for h in range(4):
                        P.bn_aggr(mv[:, h, :], bst[:, h, :])
                    P.ts("dve", grt, mv[:, :, 1], EPS, ALU.add)
                    P.tt("pool", grs, grt, mhalf, ALU.pow)
                    for j in range(2):
                        P.tt("dve", y_bf[:, j * 384:(j + 1) * 384].wb(ybufs[j]), ypb[j][:, 0:384],
                             sg[pb][:, j * 384:(j + 1) * 384], ALU.mult)
                    for h in range(4):
                        ob = obk[h // 2][:, (h % 2) * 192:(h % 2) * 192 + 192]
                        tr_ = tmpR[:, h, :].wb(tbufs[h])
                        P.stt(tr_, ob, mv[:, h, 0:1], sg[pb][:, 768 + h * 192:768 + (h + 1) * 192],
                              ALU.subtract, ALU.mult)
                        P.act(y_bf[:, 768 + h * 192:768 + (h + 1) * 192].wb(ybufs[2 + h]), tr_, AF.Copy,
                              scale=grs[:, h:h + 1])
                mem_attention(0, xqT_sb[pb], y_bf[:, 1536:2048].wb(ybufs[6]), "ev", pb, gate=sg[pb][:, 1536:2048], between=between)
                for h in range(4):
                    kb = kvb[h // 2][:, (h % 2) * 192:(h % 2) * 192 + 192]
                    P.stt(Rst[:, h, :], Rst[:, h, :], gch[h], kb, ALU.mult, ALU.add)
                P.copy("pool", Rbf, Rst)
                out_proj(y_bf_all, w_out, ht, ht, pb)
                P.dma("sp", h1_tiles[i], ht)

            rot_state["set"] = [4, 5, 6, 7]
            rot_state["n"] = 0
            for step in range(NT + 1):
                if step < NT:
                    ev_stage_a(step)
                if step >= 1:
                    ev_stage_b(step - 1)
            rot_state["set"] = list(range(8))

        if stage == "even":
            P.emit()
            return nc

        def dtiles(name, per_shape, dtype):
            t = P.dram(name, [NT] + list(per_shape), dtype)
            return [t[i].wb(P.buf(f"{name}_{i}")) for i in range(NT)]

        qT_d = dtiles("qT_d", [128, 12 * 128], BF16)
        gates_d = dtiles("gates_d", [128, 36], F32)
        sg_d = dtiles("sg_d", [128, 2048], BF16)
        ym_d = dtiles("ym_d", [128, 512], F32)
        kwT_d = dtiles("kwT_d", [128, 256], BF16)
        vw1_d = dtiles("vw1_d", [128, 2 * 129], BF16)
        F_d = P.dram("F_d", [2, 12, FLEN], BF16)

        with P.scope():
            ksT_all = P.sbuf("ksT_all", [128, 2, S], BF16)
            ksT_tiles = [ksT_all[:, :, i * 128:(i + 1) * 128].wb(P.buf(f"ksT_{i}")) for i in range(NT)]
            vs1_all = P.sbuf("vs1_all", [128, NT, 2, 129], BF16)
            P.memset("pool", vs1_all, 1.0)
            kcmpT = P.sbuf("kcmpT", [128, 2, 256], BF16)
            P.memset("pool", kcmpT, 0.0)
            vc1 = P.sbuf("vc1", [128, 2, 2, 193], BF16)
            P.memset("pool", vc1, 0.0)

            with P.scope():
                kvcT = P.sbuf("kvcT", [128, 4, S], BF16)
                kvc_bufs = [P.buf(f"kvc_{i}") for i in range(NT)]
                with P.scope():
                    w_in = P.sbuf("od_w_in_sb", [128, 8, 5668], BF16)
                    for k in range(8):
                        P.dma("pool", w_in[:, k, :], I["od_w_in"][k * 128:(k + 1) * 128, :],
                              waw=False, max_dma_last_dim=4096)
                    g_od = P.sbuf("g_od", [128, D], F32)
                    P.dma("sp", g_od, bcast_rows(I["norm_g"], 1, D))
                    print("sbuf remaining before passA work tiles", nc.sbuf_bytes_remaining)
                    hT = [P.sbuf(f"ahT{j}", [128, D], F32) for j in range(2)]
                    ath_t = P.sbuf("ath_t", [128, 512], F32)
                    ss = [P.sbuf(f"ass{j}", [128, 1], F32) for j in range(2)]
                    rtt = [P.sbuf(f"artt{j}", [128, 1], F32) for j in range(2)]
                    rstd = [P.sbuf(f"arstd{j}", [128, 1], F32) for j in range(2)]
                    u_bf = P.sbuf("au_bf", [128, D], BF16)
                    junk = u_bf
                    uTt = P.sbuf("auT", [128, 8, 128], BF16)
                    qT_sb = [P.sbuf(f"aqT{j}", [128, 12, 128], BF16) for j in range(2)]
                    kwT_sb = [P.sbuf(f"akwT{j}", [128, 2, 128], BF16) for j in range(2)]
                    vw1_sb = [P.sbuf(f"avw1{j}", [128, 2, 129], BF16) for j in range(2)]
                    for j in range(2):
                        P.memset("pool", vw1_sb[j], 1.0)
                    xqT_sb = [P.sbuf(f"axqT{j}", [128, 4, 128], BF16) for j in range(2)]
                    gates_sb = [P.sbuf(f"agates{j}", [128, 36], F32) for j in range(2)]
                    sg_sb = [P.sbuf(f"asg{j}", [128, 2048], BF16) for j in range(2)]
                    ym_sb = [P.sbuf(f"aym{j}", [128, 512], F32) for j in range(2)]
                    mem_pT = [P.sbuf("amem_pT", [128, 8, 128], BF16)] * 2
                    mem_rs = [P.sbuf("amem_rs", [128, 4], F32)] * 2

                    QS = 128.0 ** -0.5

                    P.dma("sp", hT[0], h1_tiles[0])

                    def od_head(i):
                        rms_stats(hT[i % 2], rstd[i % 2], junk, ss[i % 2], rtt[i % 2])
                        P.stt(u_bf, hT[i % 2], rstd[i % 2], g_od, ALU.mult, ALU.mult)

                    od_head(0)

                    def od_stage_a(i):
                        pb = i % 2
                        ht = hT[pb]
                        if i + 1 < NT:
                            P.dma("sp", hT[(i + 1) % 2], h1_tiles[i + 1])
                        bk = rot()
                        bb = bk.bf()
                        for k in range(8):
                            P.tr(bb[:, k * 128:(k + 1) * 128], u_bf[:, k * 128:(k + 1) * 128], ident)
                        P.copy("act", uTt.re("p k t -> p (k t)"), bb)

                        def fm_group(cols):
                            b = rot()
                            for j, c0 in enumerate(cols):
                                for k in range(8):
                                    P.mm(b[:, j * 128:(j + 1) * 128], w_in[:, k, c0:c0 + 128], uTt[:, k, :],
                                         start=(k == 0), stop=(k == 7))
                            return b

                        for qg in range(3):
                            b = fm_group([(qg * 4 + j) * 128 for j in range(4)])
                            if qg == 1:
                                P.ts("dve", qT_sb[pb][:, qg * 4:(qg + 1) * 4, :].re("p h t -> p (h t)"), b, QS, ALU.mult)
                            else:
                                P.act(qT_sb[pb][:, qg * 4:(qg + 1) * 4, :].re("p h t -> p (h t)"), b, AF.Copy, scale=QS)
                        b = fm_group([2048, 2176, 2560, 2688])
                        P.copy("dve", ksT_tiles[i], b[:, 0:256].re("p (g t) -> p g t", g=2))
                        P.copy("act", kwT_sb[pb].re("p g t -> p (g t)"), b[:, 256:512])
                        b = fm_group([1536, 1664, 1792, 1920])
                        P.copy("dve", kvcT[:, :, i * 128:(i + 1) * 128].wb(kvc_bufs[i]),
                               b.re("p (c t) -> p c t", c=4))
                        b = fm_group([3108 + h * 128 for h in range(4)])
                        P.ts("dve", xqT_sb[pb].re("p h t -> p (h t)"), b, QS, ALU.mult)

                        def proj_tok(b, o0, c0, width, first=True):
                            for k in range(8):
                                P.mm(b[:, o0:o0 + width], uTt[:, k, :], w_in[:, k, c0:c0 + width],
                                     start=(k == 0), stop=(k == 7))

                        b = rot()
                        proj_tok(b, 0, 2304, 256)
                        proj_tok(b, 256, 2816, 256)
                        P.copy("act", vs1_all[:, i, :, 0:128], b[:, 0:256].re("p (g d) -> p g d", g=2))
                        P.copy("dve", vw1_sb[pb][:, :, 0:128], b[:, 256:512].re("p (g d) -> p g d", g=2))
                        if i + 1 < NT:
                            od_head(i + 1)
                        b = rot()
                        proj_tok(b, 0, 3072, 36)
                        P.act(gates_sb[pb], b[:, 0:36], AF.Tanh, scale=0.5)
                        P.ts("dve", gates_sb[pb], gates_sb[pb], 0.5, ALU.mult, 0.5, ALU.add)
                        for j in range(4):
                            b = rot()
                            proj_tok(b, 0, 3620 + 512 * j, 512)
                            P.act(ath_t, b, AF.Tanh, scale=0.5)
                            P.stt(sg_sb[pb][:, j * 512:(j + 1) * 512], ath_t, 1.0, b, ALU.add, ALU.mult)
                        P.dma("sp", qT_d[i], qT_sb[pb].re("p h t -> p (h t)"))
                        P.dma("sp", kwT_d[i], kwT_sb[pb].re("p g t -> p (g t)"))
                        P.dma("sp", vw1_d[i], vw1_sb[pb].re("p g t -> p (g t)"))
                        P.dma("sp", gates_d[i], gates_sb[pb])
                        P.dma("sp", sg_d[i], sg_sb[pb])

                    def od_stage_b(i):
                        pb = i % 2
                        mem_attention(1, xqT_sb[pb], ym_sb[pb], "od", pb)
                        P.dma("sp", ym_d[i], ym_sb[pb])

                    for step in range(NT + 1):
                        if step < NT:
                            od_stage_a(step)
                        if step >= 1:
                            od_stage_b(step - 1)

                with P.scope():
                    kvc_all = V(kvcT.ap, kvc_bufs)
                    kvc_ds = P.sbuf("kvc_ds", [128, 4, 16, 256], BF16)
                    for c_, eng_ in enumerate(("dve", "act", "pool", "dve")):
                        P.copy(eng_, kvc_ds[:, c_, :, :], kvc_all[:, c_, :].re("p (j l) -> p l j", l=16))
                    w1 = P.sbuf("cw1", [128, 2, 32, 256], BF16)
                    for t in range(2):
                        for l0 in range(0, 32, 8):
                            P.dma("pool", w1[:, t, l0:l0 + 8, :],
                                  I["od_cmp_w1"][t, l0 * 128:(l0 + 8) * 128, :].re("(l p) h -> p l h", p=128),
                                  waw=False)
                    w2 = P.sbuf("cw2", [128, 2, 2, 128], BF16)
                    for t in range(2):
                        P.dma("pool", w2[:, t, :, :], I["od_cmp_w2"][t].re("(hc p) d -> p hc d", p=128), waw=False)
                    pe_sb = P.sbuf("pe_sb", [32, 2, 128], F32)
                    for t in range(2):
                        P.dma("sp", pe_sb[:, t, :], I["od_cmp_pe"][t], waw=False)
                    identf = P.sbuf("identf", [32, 32], F32)
                    P.dma("sp", identf, K["c_ident"][0:32, 0:32])
                    peT = P.sbuf("peT", [128, 2, 32], BF16)
                    for t in range(2):
                        bk = rot()
                        P.tr(bk[:, 0:32], pe_sb[:, t, :], identf)
                        P.copy("dve", peT[:, t, :], bk[:, 0:32])
                    b1col = P.sbuf("b1col", [128, 4], F32)
                    P.dma("sp", b1col, I["od_cmp_b1"].re("t (hc p) -> p (t hc)", p=128), allow_slow_non_contiguous=True)
                    bk = rot()
                    for t in range(2):
                        for hc in range(2):
                            col = t * 2 + hc
                            for l in range(32):
                                P.mm(bk[:, col:col + 1], w1[:, t, l, hc * 128:(hc + 1) * 128], peT[:, t, l:l + 1],
                                     start=(l == 0), stop=(l == 31))
                    b1p = P.sbuf("b1p", [128, 4], F32)
                    P.tt("dve", b1p, bk[:, 0:4], b1col, ALU.add)
                    for g in range(2):
                        P.dma("pool", vc1[:, :, g, 128:193], K["c_ovl1"])
                    h1T = [P.sbuf(f"h1T{j}", [128, 2, 256], BF16) for j in range(2)]
                    for t in range(2):
                        for g in range(2):
                            hb = rot()
                            for hc in range(2):
                                for l in range(32):
                                    P.mm(hb[:, hc * 256:hc * 256 + 255], w1[:, t, l, hc * 128:(hc + 1) * 128],
                                         kvc_ds[:, t * 2 + g, l % 16, (l // 16):(l // 16) + 255],
                                         start=(l == 0), stop=(l == 31))
                            hT_ = h1T[g]
                            for hc in range(2):
                                P.act(hT_[:, hc, 0:255], hb[:, hc * 256:hc * 256 + 255], AF.Silu,
                                      bias=b1p[:, t * 2 + hc:t * 2 + hc + 1])
                            if t == 0:
                                kb = rot()
                                for hc in range(2):
                                    P.mm(kb[:, 0:255], w2[:, 0, hc, :], hT_[:, hc, 0:255], start=(hc == 0), stop=(hc == 1))
                                P.copy("dve", kcmpT[:, g, 0:255], kb[:, 0:255])
                            else:
                                vb = rot()
                                for jt in range(2):
                                    nj = 128 if jt == 0 else 127
                                    for hc in range(2):
                                        P.mm(vb[0:nj, jt * 128:(jt + 1) * 128], hT_[:, hc, jt * 128:jt * 128 + nj],
                                             w2[:, 1, hc, :], start=(hc == 0), stop=(hc == 1))
                                    P.copy("dve", vc1[0:nj, jt, g, 0:128], vb[0:nj, jt * 128:(jt + 1) * 128])

            with P.scope():
                with P.scope():
                    tabx = P.sbuf("tabx", [33, 12], F32)
                    P.memset("dve", tabx[32:33, :], NEG)
                    P.dma("sp", tabx[0:32, :], I["rel_bias"], waw=False)
                    oh_sb = P.sbuf("oh_sb", [33, 2, FLEN], F32)
                    for var in range(2):
                        P.dma("sp", oh_sb[:, var, :], K["c_oh"][var], waw=False)
                    F_sb = P.sbuf("F_sb", [12, 2, FLEN], BF16)
                    for var in range(2):
                        for c in range(FLEN // 512):
                            bk = rot()
                            P.mm(bk[0:12, :], tabx, oh_sb[:, var, c * 512:(c + 1) * 512])
                            P.copy("dve" if c % 2 else "act", F_sb[:, var, c * 512:(c + 1) * 512], bk[0:12, :])
                    P.dma("sp", F_d.re("v h n -> h v n"), F_sb)
                rot_state["set"] = [4, 5, 6, 7]
                rot_state["n"] = 0
                ACC = banks[0:4]
                w_out = P.sbuf("od_w_out_sb", [128, 16, 1024], BF16)
                for k0 in range(0, 16, 4):
                    P.dma("pool", w_out[:, k0:k0 + 4, :],
                          I["od_w_out"][k0 * 128:(k0 + 4) * 128, :].re("(k p) c -> p k c", p=128), waw=False,
                          max_dma_last_dim=4096)
                anti = P.sbuf("anti", [128, 128], BF16)
                P.dma("pool", anti, K["c_anti"])
                E_sb = P.sbuf("E_sb", [64, 32, 128], BF16)
                for k0 in range(0, 32, 8):
                    P.dma("pool", E_sb[:, k0:k0 + 8, :], K["c_E"][:, k0:k0 + 8, :], waw=False, max_dma_last_dim=4096)
                keep_t = P.sbuf("keep_t", [128, 128], F32)
                addm_t = P.sbuf("addm_t", [128, 128], F32)
                P.dma("sp", keep_t, K["c_keep"])
                P.dma("sp", addm_t, K["c_addm"])
                g_fin = P.sbuf("g_fin", [128, D], F32)
                P.dma("sp", g_fin, bcast_rows(I["final_g"], 0, D))

                def hankel(var, off, pstep):
                    ap = bass.AP(F_d.ap.tensor, var * 12 * FLEN + off, [[pstep, 128], [FLEN, 12], [1, 128]])
                    return V(ap, F_d.bufs)

                Tb = {}
                for dl, var in ((0, 0), (128, 0), (512, 1)):
                    Tb[dl] = P.sbuf(f"Tb{dl}", [128, 12, 128], BF16)
                    P.dma("sp", Tb[dl], hankel(var, FOFF + dl - 127, 1))
                cbias = [[P.sbuf(f"cbias{j}_{jt}", [128, 12, 128], BF16) for jt in range(2)] for j in range(2)]
                kw_ring = P.sbuf("kw_ring", [128, 6, 2, 128], BF16)
                vw_ring = P.sbuf("vw_ring", [128, 6, 2, 129], BF16)
                kw_slots = [kw_ring[:, s_, :, :].wb(P.buf(f"kwslot{s_}")) for s_ in range(6)]
                vw_slots = [vw_ring[:, s_, :, :].wb(P.buf(f"vwslot{s_}")) for s_ in range(6)]
                print("sbuf remaining before passB work tiles", nc.sbuf_bytes_remaining)
                hT = [P.sbuf(f"bhT{j}", [128, D], F32) for j in range(2)]
                junk = P.sbuf("bjunk", [128, D], BF16)
                ss = [P.sbuf(f"bss{j}", [128, 1], F32) for j in range(2)]
                rtt = [P.sbuf(f"brtt{j}", [128, 1], F32) for j in range(2)]
                rstd = [P.sbuf(f"brstd{j}", [128, 1], F32) for j in range(2)]
                qT_sb = [P.sbuf(f"bqT{j}", [128, 12, 128], BF16) for j in range(2)]
                gates_sb = [P.sbuf(f"bgates{j}", [128, 36], F32) for j in range(2)]
                sg_sb = [P.sbuf(f"bsg{j}", [128, 2048], BF16) for j in range(2)]
                y_pre = [P.sbuf(f"by_pre{j}", [128, 2048], F32) for j in range(2)]
                y_bf = [P.sbuf(f"by_bf{j}", [128, 2048], BF16) for j in range(2)]
                ybB = [[P.buf(f"ybB{j}_{k}") for k in range(4)] for j in range(2)]
                yT_t = [P.sbuf("byT", [128, 16, 128], BF16)] * 2
                out_t = [P.sbuf(f"bout{j}", [128, D], F32) for j in range(2)]
                pTs = [P.sbuf(f"bpT{j}", [128, 3, 128], BF16) for j in range(4)]
                pt_state = {"n": 0}

                def next_pT():
                    t_ = pTs[pt_state["n"] % 4]
                    pt_state["n"] += 1
                    return t_

                zt = [P.sbuf(f"bzt{j}", [128, 6], F32) for j in range(2)]
                rz = [P.sbuf(f"brz{j}", [128, 6], F32) for j in range(2)]
                aco = [P.sbuf(f"baco{j}", [128, 6], F32) for j in range(2)]
                imp = [P.sbuf(f"bimp{j}", [128, 64], F32) for j in range(2)]
                imp2 = [P.sbuf(f"bimp2{j}", [128, 64], F32) for j in range(2)]
                m8 = [P.sbuf(f"bm8{j}", [128, 8], F32) for j in range(2)]
                negsel = [P.sbuf(f"bnegsel{j}", [128, 64], BF16) for j in range(2)]
                selT = [P.sbuf(f"bselT{j}", [64, 128], BF16) for j in range(2)]
                maskT = [P.sbuf(f"bmaskT{j}", [128, NT, 128], BF16) for j in range(2)]

                def branch_scales(accs, ncols, nper, gate_cols, gi):
                    for bi in range(6 // nper):
                        zsrc = accs[bi][:, 128:128 + ncols * (nper - 1) + 1:ncols]
                        P.ts("dve", zt[gi][:, bi * nper:(bi + 1) * nper], zsrc, 1e-30, ALU.add)
                    P.recip(rz[gi], zt[gi])
                    P.tt("dve", aco[gi], rz[gi], gate_cols, ALU.mult)

                NPT = 9
                pTs2 = [P.sbuf(f"bpTx{j}", [128, 3, 128], BF16) for j in range(NPT)]
                pipe_q = []
                LOOK = 7

                def pipe_push(fn):
                    pipe_q.append(fn)
                    while len(pipe_q) > LOOK:
                        pipe_q.pop(0)()

                def pipe_flush():
                    while pipe_q:
                        pipe_q.pop(0)()

                def next_pT2():
                    t_ = pTs2[pt_state["n"] % NPT]
                    pt_state["n"] += 1
                    return t_

                def unit(lhsT_k, q3, extra, accs_dst, rhs_v, starts, stop, mask=None):
                    sbk = rot()
                    n_e = len(extra)
                    P.mm(sbk[:, 0:384], lhsT_k, q3, start=True, stop=(n_e == 0), skip_group_check=True)
                    for j, (l_, r_) in enumerate(extra):
                        P.mm(sbk[:, 0:384], l_, r_, start=False, stop=(j == n_e - 1), skip_group_check=True)
                    pT = next_pT2()
                    P.act(pT.re("p h t -> p (h t)"), sbk[:, 0:384], AF.Exp)
                    if mask is not None:
                        P.tt("dve", pT, pT, mask, ALU.mult)

                    def s2():
                        for hd3 in range(3):
                            P.mm(accs_dst[hd3], pT[:, hd3, :], rhs_v, start=starts[hd3], stop=stop,
                                 skip_group_check=True)
                    pipe_push(s2)

                for i in range(NT):
                    pb = i % 2
                    ht = hT[pb]
                    P.dma("sp", ht, h1_tiles[i])
                    P.dma("sp", qT_sb[pb].re("p h t -> p (h t)"), qT_d[i])
                    P.dma("sp", gates_sb[pb], gates_d[i])
                    P.dma("sp", sg_sb[pb], sg_d[i])
                    P.dma("sp", y_pre[pb][:, 1536:2048], ym_d[i])
                    P.tt("pool", y_bf[pb][:, 1536:2048].wb(ybB[pb][3]), y_pre[pb][:, 1536:2048],
                         sg_sb[pb][:, 1536:2048], ALU.mult)
                    P.dma("sp", kw_slots[i % 6].re("p g t -> p (g t)"), kwT_d[i])
                    P.dma("sp", vw_slots[i % 6].re("p g t -> p (g t)"), vw1_d[i])
                    jts = [0] + ([1] if i >= 16 else [])
                    need_cb = {}
                    for jt in jts:
                        need_cb[jt] = (jt == 1) or (i <= 17)
                        if need_cb[jt]:
                            P.dma("sp", cbias[pb][jt], hankel(0, FOFF + 128 * i - 2048 * jt - 31 - 2032, 16))
                    qf = qT_sb[pb]
                    for g in range(2):
                        gi = g

                        def q3of(hh, g=g, qf=qf):
                            return qf[:, g * 6 + hh * 3:g * 6 + hh * 3 + 3, :].re("p h t -> p (h t)")

                        def hsl(tile_, hh, g=g):
                            return tile_[:, g * 6 + hh * 3:g * 6 + hh * 3 + 3, :].re("p h t -> p (h t)")

                        CB = [2, 3, 0] if g == 0 else [0, 1, 2]
                        cacc = [ACC[b_] for b_ in CB]
                        early = [hd for hd in range(6) if CB[hd // 2] in (2, 3)]
                        late = [hd for hd in range(6) if CB[hd // 2] not in (2, 3)]
                        first_in_bank = {0: True, 1: True, 2: True}
                        for jt in jts:
                            for hh in range(2):
                                extra = []
                                if need_cb[jt]:
                                    extra.append((anti, hsl(cbias[pb][jt], hh)))
                                dsts, sts = [], []
                                for hd3 in range(3):
                                    hd = hh * 3 + hd3
                                    bi = hd // 2
                                    dsts.append(cacc[bi][:, (hd % 2) * 193:(hd % 2) * 193 + 193])
                                    sts.append(first_in_bank[bi])
                                    first_in_bank[bi] = False
                                unit(kcmpT[:, g, jt * 128:(jt + 1) * 128], q3of(hh), extra, dsts,
                                     vc1[:, jt, g, :], sts, jt == jts[-1])

                        def cmp_evac(g=g, gi=gi, pb=pb, i=i, cacc=cacc, early=early, late=late):
                            branch_scales(cacc, 193, 2, gates_sb[pb][:, g * 6:g * 6 + 6], gi)

                            def imp_acc(hd, first):
                                src = cacc[hd // 2][:, (hd % 2) * 193 + 129:(hd % 2) * 193 + 193]
                                if first:
                                    P.ts("dve", imp[gi], src, rz[gi][:, hd:hd + 1], ALU.mult)
                                else:
                                    P.stt(imp[gi], src, rz[gi][:, hd:hd + 1], imp[gi], ALU.mult, ALU.add)

                            def y_out(hd):
                                src = cacc[hd // 2][:, (hd % 2) * 193:(hd % 2) * 193 + 128]
                                P.ts("dve", y_pre[pb][:, (g * 6 + hd) * 128:(g * 6 + hd + 1) * 128], src,
                                     aco[gi][:, hd:hd + 1], ALU.mult)

                            for n_, hd in enumerate(early):
                                imp_acc(hd, n_ == 0)
                            for hd in early:
                                y_out(hd)
                            for hd in late:
                                imp_acc(hd, False)
                            P.tt("dve", imp2[gi], imp[gi], keep_t[:, 64 - 2 * i:128 - 2 * i], ALU.mult)
                            P.tt("dve", imp2[gi], imp2[gi], addm_t[:, 64 - 2 * i:128 - 2 * i], ALU.add)
                            P.memset("dve", imp2[gi][:, 0:1], BIGV)
                            P.max8(m8[gi], imp2[gi])
                            P.ts("dve", negsel[gi], imp2[gi], m8[gi][:, 7:8], ALU.is_ge)
                            for hd in late:
                                y_out(hd)
                        pipe_push(cmp_evac)

                        def cmp_mask_T(gi=gi, i=i):
                            tb = rot()
                            tbb = tb.bf()
                            P.tr(tbb[0:64, 0:128], negsel[gi], ident)
                            P.copy("dve", selT[gi], tbb[0:64, 0:128])
                            for k0 in range(0, i + 1, 4):
                                nk = min(4, i + 1 - k0)
                                mb = rot()
                                for kk in range(nk):
                                    P.mm(mb[:, kk * 128:(kk + 1) * 128], E_sb[:, k0 + kk, :], selT[gi])
                                P.copy("act" if (k0 // 4) % 2 == 0 else "dve",
                                       maskT[gi][:, k0:k0 + nk, :].re("p k t -> p (k t)"), mb[:, 0:nk * 128])

                        kt0 = max(0, i - 4)
                        for kt in range(kt0, i + 1):
                            dl = 128 * (i - kt)
                            for hh in range(2):
                                extra = []
                                if dl in (0, 128, 512):
                                    extra.append((anti, hsl(Tb[dl], hh)))
                                dsts = [ACC[2 + hh][:, hd3 * 129:hd3 * 129 + 129] for hd3 in range(3)]
                                sts = [(kt == kt0 and hd3 == 0) for hd3 in range(3)]
                                unit(kw_slots[kt % 6][:, g, :], q3of(hh), extra, dsts, vw_slots[kt % 6][:, g, :],
                                     sts, kt == i)

                        def win_evac(g=g, gi=gi, pb=pb):
                            branch_scales(ACC[2:4], 129, 3, gates_sb[pb][:, 24 + g * 6:24 + g * 6 + 6], gi)
                            for hd in range(6):
                                src = ACC[2 + hd // 3][:, (hd % 3) * 129:(hd % 3) * 129 + 128]
                                yv = y_pre[pb][:, (g * 6 + hd) * 128:(g * 6 + hd + 1) * 128]
                                P.stt(yv, src, aco[gi][:, hd:hd + 1], yv, ALU.mult, ALU.add)
                        pipe_push(cmp_mask_T)
                        pipe_push(win_evac)

                    pipe_flush()
                    for g in range(2):
                        gi = g

                        def q3of(hh, g=g, qf=qf):
                            return qf[:, g * 6 + hh * 3:g * 6 + hh * 3 + 3, :].re("p h t -> p (h t)")

                        def hsl(tile_, hh, g=g):
                            return tile_[:, g * 6 + hh * 3:g * 6 + hh * 3 + 3, :].re("p h t -> p (h t)")

                        for hh in range(2):
                            for kt in range(i + 1):
                                dl = 128 * (i - kt)
                                extra = []
                                if dl <= 128:
                                    extra.append((anti, hsl(Tb[dl], hh)))
                                dsts = [ACC[hh][:, hd3 * 129:hd3 * 129 + 129] for hd3 in range(3)]
                                sts = [(kt == 0 and hd3 == 0) for hd3 in range(3)]
                                unit(ksT_all[:, g, kt * 128:(kt + 1) * 128], q3of(hh), extra, dsts,
                                     vs1_all[:, kt, g, :], sts, kt == i,
                                     mask=maskT[gi][:, kt:kt + 1, :].bcast([128, 3, 128]))

                            def sel_evac(g=g, gi=gi, pb=pb, hh=hh):
                                off = hh * 3
                                zsrc = ACC[hh][:, 128:128 + 129 * 2 + 1:129]
                                P.ts("dve", zt[gi][:, off:off + 3], zsrc, 1e-30, ALU.add)
                                P.recip(rz[gi][:, off:off + 3], zt[gi][:, off:off + 3])
                                P.tt("dve", aco[gi][:, off:off + 3], rz[gi][:, off:off + 3],
                                     gates_sb[pb][:, 12 + g * 6 + off:12 + g * 6 + off + 3], ALU.mult)
                                for hd3 in range(3):
                                    hd = off + hd3
                                    src = ACC[hh][:, hd3 * 129:hd3 * 129 + 128]
                                    yv = y_pre[pb][:, (g * 6 + hd) * 128:(g * 6 + hd + 1) * 128]
                                    P.stt(yv, src, aco[gi][:, hd:hd + 1], yv, ALU.mult, ALU.add)
                                if g == 0 and hh == 1:
                                    P.tt("pool", y_bf[pb][:, 0:768].wb(ybB[pb][0]), y_pre[pb][:, 0:768],
                                         sg_sb[pb][:, 0:768], ALU.mult)
                                if g == 1:
                                    c0 = 768 + hh * 384
                                    P.tt("dve", y_bf[pb][:, c0:c0 + 384].wb(ybB[pb][1 + hh]), y_pre[pb][:, c0:c0 + 384],
                                         sg_sb[pb][:, c0:c0 + 384], ALU.mult)
                            pipe_push(sel_evac)

                    def tile_epilogue(i=i, pb=pb, ht=ht):
                        out_proj(V(y_bf[pb].ap, ybB[pb]), w_out, ht, ht, pb)
                        rms_stats(ht, rstd[pb], junk, ss[pb], rtt[pb])
                        P.stt(out_t[pb], ht, rstd[pb], g_fin, ALU.mult, ALU.mult)
                        P.dma("sp", out_d[i * 128:(i + 1) * 128, :], out_t[pb], waw=False)
                    pipe_push(tile_epilogue)
                pipe_flush()

    P.emit()
    return nc


_PROG = {}


def _get_prog(stage):
    if stage not in _PROG:
        _PROG[stage] = build_program(stage)
    return _PROG[stage]


def _in_maps(inputs):
    C = make_consts()
    shared = {}
    for k, shp in IN_SHAPES.items():
        if k in ("x", "mem"):
            continue
        a = np.asarray(inputs[k], dtype=np.float32)
        if k in ("norm_g",):
            a = a.reshape(shp)
        elif a.ndim >= 1 and a.shape[0] == 1 and list(a.shape[1:]) == shp[-(a.ndim - 1):] and a.ndim - 1 == len(shp):
            a = a[0]
        a = np.ascontiguousarray(a.reshape(shp))
        shared[k] = a
    for k, shp in CONST_SHAPES.items():
        shared[k] = np.ascontiguousarray(C[k].reshape(shp).astype(np.float32))
    maps = []
    x = np.asarray(inputs["x"], dtype=np.float32)
    mem = np.asarray(inputs["mem"], dtype=np.float32)
    for b in range(8):
        m = dict(shared)
        m["x"] = np.ascontiguousarray(x[b])
        m["mem"] = np.ascontiguousarray(mem[b])
        maps.append(m)
    return maps


def run_stage(inputs, stage, cores=8):
    nc = _get_prog(stage)
    maps = _in_maps(inputs)[:cores]
    res = run_bass_kernel_spmd(nc, maps, core_ids=list(range(cores)))
    return res.results


def kernel(**inputs):
    res = run_stage(inputs, "full")
    return np.stack([r["out"] for r in res], axis=0).astype(np.float32)
```
